# Optimizing a Trainium2 kernel written in Bass

```python
import math
import jax, jax.numpy as jnp
from jax import lax
import numpy as np

D_MODEL = 1024
BATCH = 4
SEQ = 8192
DEPTH = 1

D_MIX = D_MODEL
ATTN_HEADS = 4
ATTN_QK_DIM = 64
ATTN_V_DIM = 2 * ATTN_QK_DIM
ATTN_WIDTH = ATTN_HEADS * ATTN_V_DIM
ATTN_QK_WIDTH = 2 * ATTN_HEADS * ATTN_QK_DIM
RNN_WIDTH = D_MIX - ATTN_WIDTH
RNN_BLOCKS = 8
RNN_BLOCK_W = RNN_WIDTH // RNN_BLOCKS
CONV_WIDTH = 4
RG_LRU_C = 8.0
D_FF = 2816
N_BUCKETS = 32
MAX_DISTANCE = 128
Q_BLOCK = 128
NORM_EPS = 1e-6
N_MOD = 9
IN_SPLITS = (ATTN_QK_WIDTH, 2 * ATTN_QK_WIDTH, 2 * ATTN_QK_WIDTH + ATTN_WIDTH,
             2 * ATTN_QK_WIDTH + ATTN_WIDTH + RNN_WIDTH)
D_IN = 2 * ATTN_QK_WIDTH + ATTN_WIDTH + 2 * RNN_WIDTH

kernel_name = "hymba_style_diffattn_rglru_macaron_adaln"


def rms_norm(x, g):
    xf = x.astype(jnp.float32)
    y = xf * lax.rsqrt(jnp.mean(xf * xf, axis=-1, keepdims=True) + NORM_EPS)
    return (y * g.astype(jnp.float32)).astype(x.dtype)


def modulate(h, shift, scale):
    return h * (1 + scale[:, None, :]) + shift[:, None, :]


def swiglu(h, w1, w3, w2):
    return jnp.einsum('bsf,fd->bsd', jax.nn.silu(h @ w1) * (h @ w3), w2)


def t5_bucket(rel):
    n = jnp.maximum(rel, 0)
    max_exact = N_BUCKETS // 2
    nf = jnp.maximum(n, 1).astype(jnp.float32)
    large = max_exact + (jnp.log(nf / max_exact) / math.log(MAX_DISTANCE / max_exact)
                         * (N_BUCKETS - max_exact)).astype(jnp.int32)
    large = jnp.minimum(large, N_BUCKETS - 1)
    return jnp.where(n < max_exact, n, large)


def diff_attention(q, k, v, rel_bias, lam):
    B, _, S, _ = q.shape
    scale = ATTN_QK_DIM ** -0.5
    kpos = jnp.arange(S, dtype=jnp.int32)

    def block(start):
        qb = lax.dynamic_slice_in_dim(q, start, Q_BLOCK, axis=2)
        s = jnp.einsum('bmqd,bmkd->bmqk', qb, k,
                       preferred_element_type=jnp.float32) * scale
        rel = (start + jnp.arange(Q_BLOCK, dtype=jnp.int32))[:, None] - kpos[None, :]
        bias = jnp.take(rel_bias, t5_bucket(rel), axis=0)
        s = s + jnp.transpose(bias, (2, 0, 1)).astype(jnp.float32)[None]
        s = jnp.where((rel >= 0)[None, None], s, -jnp.inf)
        p = jax.nn.softmax(s, axis=-1).reshape(B, ATTN_HEADS, 2, Q_BLOCK, S)
        a = p[:, :, 0] - lam * p[:, :, 1]
        return jnp.einsum('bhqk,bhkv->bhqv', a.astype(v.dtype), v)

    starts = jnp.arange(S // Q_BLOCK, dtype=jnp.int32) * Q_BLOCK
    o = lax.map(block, starts)
    return jnp.transpose(o, (1, 0, 3, 2, 4)).reshape(B, S, ATTN_HEADS, ATTN_V_DIM)


def causal_depthwise_conv(x, w, b):
    C = x.shape[-1]
    y = lax.conv_general_dilated(x, w[:, None, :], window_strides=(1,),
                                 padding=[(CONV_WIDTH - 1, 0)],
                                 dimension_numbers=('NWC', 'WIO', 'NWC'),
                                 feature_group_count=C)
    return y + b


def rg_lru(x, w_a, b_a, w_i, b_i, lam_L):
    B, S, C = x.shape
    xb = x.reshape(B, S, RNN_BLOCKS, RNN_BLOCK_W)
    r = jax.nn.sigmoid(jnp.einsum('bsnc,ncd->bsnd', xb, w_a).reshape(B, S, C) + b_a).astype(jnp.float32)
    i = jax.nn.sigmoid(jnp.einsum('bsnc,ncd->bsnd', xb, w_i).reshape(B, S, C) + b_i).astype(jnp.float32)
    log_a = RG_LRU_C * r * jax.nn.log_sigmoid(lam_L.astype(jnp.float32))
    a = jnp.exp(log_a)
    u = jnp.sqrt(-jnp.expm1(2.0 * log_a)) * (i * x.astype(jnp.float32))

    def combine(left, right):
        a1, b1 = left
        a2, b2 = right
        return a1 * a2, a2 * b1 + b2

    _, h = lax.associative_scan(combine, (a, u), axis=1)
    return h.astype(x.dtype)


def setup_inputs(seed: int = 0) -> dict:
    key = jax.random.key(seed)
    ks = jax.random.split(key, 32)
    f32 = jnp.float32
    nrm = lambda k, shape, s: jax.random.normal(k, shape, f32) * s
    u = jax.random.uniform(ks[21], (DEPTH, RNN_WIDTH), f32, 0.9, 0.999)
    s_a = u ** (1.0 / RG_LRU_C)
    lru_L = jnp.log(s_a) - jnp.log1p(-s_a)
    return {
        "x": nrm(ks[0], (BATCH, SEQ, D_MODEL), 1.0),
        "c": nrm(ks[1], (BATCH, D_MODEL), 1.0),
        "rel_bias": nrm(ks[2], (N_BUCKETS, 2 * ATTN_HEADS), 0.5),
        "ada_w": nrm(ks[3], (DEPTH, D_MODEL, N_MOD * D_MODEL), 0.5 * D_MODEL ** -0.5),
        "ada_b": nrm(ks[4], (DEPTH, N_MOD * D_MODEL), 0.02),
        "norm_g": 1.0 + nrm(ks[5], (DEPTH, 3, D_MODEL), 0.02),
        "ffn1_w1": nrm(ks[6], (DEPTH, D_MODEL, D_FF), D_MODEL ** -0.5),
        "ffn1_w3": nrm(ks[7], (DEPTH, D_MODEL, D_FF), D_MODEL ** -0.5),
        "ffn1_w2": nrm(ks[8], (DEPTH, D_FF, D_MODEL), D_FF ** -0.5),
        "w_in": nrm(ks[9], (DEPTH, D_MODEL, D_IN), D_MODEL ** -0.5),
        "lam_q1": nrm(ks[10], (DEPTH, ATTN_QK_DIM), 0.1),
        "lam_k1": nrm(ks[11], (DEPTH, ATTN_QK_DIM), 0.1),
        "lam_q2": nrm(ks[12], (DEPTH, ATTN_QK_DIM), 0.1),
        "lam_k2": nrm(ks[13], (DEPTH, ATTN_QK_DIM), 0.1),
        "subln_g": 1.0 + nrm(ks[14], (DEPTH, ATTN_V_DIM), 0.02),
        "conv_w": nrm(ks[15], (DEPTH, CONV_WIDTH, RNN_WIDTH), CONV_WIDTH ** -0.5),
        "conv_b": nrm(ks[16], (DEPTH, RNN_WIDTH), 0.01),
        "gate_a_w": nrm(ks[17], (DEPTH, RNN_BLOCKS, RNN_BLOCK_W, RNN_BLOCK_W), RNN_BLOCK_W ** -0.5),
        "gate_a_b": nrm(ks[18], (DEPTH, RNN_WIDTH), 0.01),
        "gate_i_w": nrm(ks[19], (DEPTH, RNN_BLOCKS, RNN_BLOCK_W, RNN_BLOCK_W), RNN_BLOCK_W ** -0.5),
        "gate_i_b": nrm(ks[20], (DEPTH, RNN_WIDTH), 0.01),
        "lru_L": lru_L,
        "w_out": nrm(ks[22], (DEPTH, D_MIX, D_MODEL), D_MIX ** -0.5),
        "ffn2_w1": nrm(ks[23], (DEPTH, D_MODEL, D_FF), D_MODEL ** -0.5),
        "ffn2_w3": nrm(ks[24], (DEPTH, D_MODEL, D_FF), D_MODEL ** -0.5),
        "ffn2_w2": nrm(ks[25], (DEPTH, D_FF, D_MODEL), D_FF ** -0.5),
        "final_g": 1.0 + nrm(ks[26], (D_MODEL,), 0.02),
    }


def reference(x, c, rel_bias, ada_w, ada_b, norm_g, ffn1_w1, ffn1_w3, ffn1_w2, w_in,
              lam_q1, lam_k1, lam_q2, lam_k2, subln_g, conv_w, conv_b,
              gate_a_w, gate_a_b, gate_i_w, gate_i_b, lru_L, w_out,
              ffn2_w1, ffn2_w3, ffn2_w2, final_g):
    B, S, _ = x.shape
    c_act = jax.nn.silu(c)
    for l in range(DEPTH):
        mod = jnp.einsum('bd,de->be', c_act, ada_w[l]) + ada_b[l]
        sh1, sc1, g1, sh2, sc2, g2, sh3, sc3, g3 = jnp.split(mod, N_MOD, axis=-1)

        h = modulate(rms_norm(x, norm_g[l, 0]), sh1, sc1)
        x = x + 0.5 * g1[:, None, :] * swiglu(h, ffn1_w1[l], ffn1_w3[l], ffn1_w2[l])

        h = modulate(rms_norm(x, norm_g[l, 1]), sh2, sc2)
        proj = h @ w_in[l]
        q, k, v, xr, gr = jnp.split(proj, IN_SPLITS, axis=-1)
        q = q.reshape(B, S, 2 * ATTN_HEADS, ATTN_QK_DIM).transpose(0, 2, 1, 3)
        k = k.reshape(B, S, 2 * ATTN_HEADS, ATTN_QK_DIM).transpose(0, 2, 1, 3)
        v = v.reshape(B, S, ATTN_HEADS, ATTN_V_DIM).transpose(0, 2, 1, 3)

        lambda_init = 0.8 - 0.6 * math.exp(-0.3 * l)
        lam = (jnp.exp(jnp.sum(lam_q1[l].astype(jnp.float32) * lam_k1[l].astype(jnp.float32)))
               - jnp.exp(jnp.sum(lam_q2[l].astype(jnp.float32) * lam_k2[l].astype(jnp.float32)))
               + lambda_init)
        o = diff_attention(q, k, v, rel_bias, lam)
        o = (rms_norm(o, subln_g[l]) * (1 - lambda_init)).reshape(B, S, ATTN_WIDTH)

        xr = causal_depthwise_conv(xr, conv_w[l], conv_b[l])
        hr = rg_lru(xr, gate_a_w[l], gate_a_b[l], gate_i_w[l], gate_i_b[l], lru_L[l])
        yr = hr * jax.nn.gelu(gr)

        mix = jnp.concatenate([o, yr], axis=-1) @ w_out[l]
        x = x + g2[:, None, :] * mix

        h = modulate(rms_norm(x, norm_g[l, 2]), sh3, sc3)
        x = x + 0.5 * g3[:, None, :] * swiglu(h, ffn2_w1[l], ffn2_w3[l], ffn2_w2[l])

    return rms_norm(x, final_g)
```

```python
import math
from contextlib import ExitStack

import numpy as np
import concourse.bass as bass
import concourse.mybir as mybir
from concourse.bass_utils import run_bass_kernel_spmd

F32 = mybir.dt.float32
BF16 = mybir.dt.bfloat16
AF = mybir.ActivationFunctionType
ALU = mybir.AluOpType
AX = mybir.AxisListType

D = 1024
DFF = 2816
NF = 22
KC = 8
DIN = 2560
EPS = 1e-6
NEG = -30000.0
LAMBDA_INIT = 0.8 - 0.6 * math.exp(0.0)


class Sem:
    def __init__(self, h, name):
        self.h = h
        self.n = 0
        self.name = name


class Buf:
    __slots__ = ("w", "r", "name")

    def __init__(self, name=""):
        self.w = {}
        self.r = {}
        self.name = name


class Prog:
    ENG = ("pe", "act", "dve", "pool", "sp")

    def __init__(self, nc, st, n_dma_sems=66):
        self.nc = nc
        self.st = st
        self.ops = {k: [] for k in self.ENG}
        self.esem = {}
        self.allsems = []
        self.seen = {k: {} for k in self.ENG}
        self.dsems = []
        for i in range(n_dma_sems):
            s = Sem(st.enter_context(nc.semaphore("ds%d" % i)), "d%d" % i)
            self.dsems.append(s)
            self.allsems.append(s)
        self.dnext = 0
        self.phase = 0
        self.new_engine_sems()

    def new_engine_sems(self):
        for k in self.ENG:
            s = Sem(self.st.enter_context(self.nc.semaphore("es%d_%s" % (self.phase, k))), k)
            self.esem[k] = s
            self.allsems.append(s)
        self.phase += 1

    def dma_sem(self):
        s = self.dsems[self.dnext]
        self.dnext += 1
        return s

    def op(self, eng, fn, reads=(), writes=(), sem=None, inc=1):
        waits = {}
        for b in reads:
            for s, v in b.w.items():
                if waits.get(s, 0) < v:
                    waits[s] = v
        for b in writes:
            for s, v in b.w.items():
                if waits.get(s, 0) < v:
                    waits[s] = v
            for s, v in b.r.items():
                if waits.get(s, 0) < v:
                    waits[s] = v
        seen = self.seen[eng]
        own = self.esem[eng]
        wl = []
        for s, v in waits.items():
            if eng == "pe" and s is own:
                continue
            if seen.get(s, 0) >= v:
                continue
            seen[s] = v
            wl.append((s.h, v))
        S = sem if sem is not None else own
        S.n += inc
        self.ops[eng].append((wl, fn, S.h, inc))
        for b in reads:
            if b.r.get(S, 0) < S.n:
                b.r[S] = S.n
        for b in writes:
            b.w = {S: S.n}
            b.r = {}
        return (S, S.n)

    def dma(self, eng, out, in_, reads=(), writes=(), sem=None, **kw):
        return self.op(eng, lambda e: e.dma_start(out=out, in_=in_, **kw),
                       reads=reads, writes=writes, sem=sem, inc=16)

    def dma_group(self, sem, items):
        bufs = []
        for it_ in items:
            eng, out, in_, writes = it_[:4]
            reads = it_[4] if len(it_) > 4 else ()
            self.dma(eng, out, in_, reads=reads, writes=writes, sem=sem)
            bufs.extend(writes)
        for b in bufs:
            b.w = {sem: sem.n}

    def barrier(self):
        evs = [(s, s.n) for s in self.allsems if s.n > 0]
        for k in self.ENG:
            seen = self.seen[k]
            wl = []
            for s, v in evs:
                if seen.get(s, 0) >= v:
                    continue
                seen[s] = v
                wl.append((s.h, v))
            if wl:
                self.ops[k].append((wl, None, None, 0))

    def flush(self):
        nc = self.nc
        ops = self.ops
        self.ops = {k: [] for k in self.ENG}

        def run(e, lst):
            for wl, fn, sh, inc in lst:
                for h, v in wl:
                    e.wait_ge(h, v)
                if fn is not None:
                    fn(e).then_inc(sh, inc)

        with nc.Block() as block:
            @block.tensor
            def _(e):
                run(e, ops["pe"])

            @block.scalar
            def _(e):
                run(e, ops["act"])

            @block.vector
            def _(e):
                run(e, ops["dve"])

            @block.gpsimd
            def _(e):
                run(e, ops["pool"])

            @block.sync
            def _(e):
                run(e, ops["sp"])

    def end_phase(self):
        self.barrier()
        self.flush()
        self.new_engine_sems()


def build(NBLK=64, phases=("p0", "p1", "p2a", "p2b", "p3", "p4"), debug=False):
    S = NBLK * 128
    NOWN = NBLK // 2
    SOWN = NOWN * 128
    NG = NBLK // 8
    nc = bass.Bass("TRN2", target_bir_lowering=False)

    def din(name, shape, dt=F32):
        return nc.dram_tensor(name, list(shape), dt, kind="ExternalInput").ap()

    xs = din("xs", [S, D])
    m0 = din("m0", [128, 128])
    cvec = din("cvec", [128, KC])
    ada_w = din("ada_w", [D, 9 * D])
    ada_b_pp = din("ada_b_pp", [128, 72])
    ada_b_gate = din("ada_b_gate", [128, 3, D])
    norm_g_pp = din("norm_g_pp", [128, 3, KC])
    final_g_b = din("final_g_b", [128, D])
    f1w1 = din("f1w1", [D, DFF]); f1w3 = din("f1w3", [D, DFF]); f1w2 = din("f1w2", [DFF, D])
    f2w1 = din("f2w1", [D, DFF]); f2w3 = din("f2w3", [D, DFF]); f2w2 = din("f2w2", [DFF, D])
    w_in = din("w_in", [D, DIN])
    w_out = din("w_out", [D, D])
    lamv = din("lamv", [128, 4, 64])
    subg_b = din("subg_b", [128, 128])
    conv_w_pp = din("conv_w_pp", [128, 4, 4])
    rnn_pp = din("rnn_pp", [128, 4, 4])
    ga_bd = din("ga_bd", [128, 4, 128])
    gi_bd = din("gi_bd", [128, 4, 128])
    bdm = din("bdm", [128, 8, 128])
    bpm = din("bpm", [128, 8, 128])
    bp0m = din("bp0m", [128, 8, 128])
    cm_in = din("cm", [128, 8])
    col0_in = din("col0", [128, 1])

    out = nc.dram_tensor("out", [SOWN, D], F32, kind="ExternalOutput").ap()
    x1_s = nc.dram_tensor("x1_s", [S, D], F32, kind="Internal").ap()
    x2_s = nc.dram_tensor("x2_s", [SOWN, D], F32, kind="Internal").ap()
    yr_s = nc.dram_tensor("yr_s", [512, SOWN], BF16, kind="Internal").ap()
    qT_s = nc.dram_tensor("qT_s", [512, SOWN], BF16, kind="Internal").ap()
    gate_s = nc.dram_tensor("gate_s", [3, 128, D], F32, kind="Internal").ap()
    w1s = nc.dram_tensor("w1s", [D, DFF], BF16, kind="Internal").ap()
    w3s = nc.dram_tensor("w3s", [D, DFF], BF16, kind="Internal").ap()
    w2s = nc.dram_tensor("w2s", [DFF, D], BF16, kind="Internal").ap()
    dbg = {}
    if debug:
        dbg["x1"] = nc.dram_tensor("dbg_x1", [S, D], F32, kind="ExternalOutput").ap()
        dbg["x2"] = nc.dram_tensor("dbg_x2", [SOWN, D], F32, kind="ExternalOutput").ap()
        dbg["yr"] = nc.dram_tensor("dbg_yr", [512, SOWN], BF16, kind="ExternalOutput").ap()
        dbg["qT"] = nc.dram_tensor("dbg_qT", [512, SOWN], BF16, kind="ExternalOutput").ap()
        dbg["kT"] = nc.dram_tensor("dbg_kT", [128, 4, S], BF16, kind="ExternalOutput").ap()
        dbg["v"] = nc.dram_tensor("dbg_v", [128, NBLK, 4, 129], BF16, kind="ExternalOutput").ap()
        dbg["mod"] = nc.dram_tensor("dbg_mod", [128, 6, KC], F32, kind="ExternalOutput").ap()
        dbg["gate"] = nc.dram_tensor("dbg_gate", [3, 128, D], F32, kind="ExternalOutput").ap()
        dbg["oall"] = nc.dram_tensor("dbg_oall", [SOWN, 512], F32, kind="ExternalOutput").ap()

    with ExitStack() as st:
        P = Prog(nc, st)

        def sb(stk, name, shape, dt=F32):
            return stk.enter_context(nc.sbuf_tensor("sb_" + name, list(shape), dt))

        ident = sb(st, "ident", [128, 128]); identB = Buf("ident")
        gm = sb(st, "gm", [128, 3, KC]); gmB = Buf("gm")
        shv = sb(st, "shv", [128, 3, KC]); shB = Buf("shv")
        neglam = sb(st, "neglam", [128, 1]); neglamB = Buf("neglam")
        neghalf4 = sb(st, "neghalf", [128, 4]); neghalfB = Buf("neghalf")
        neghalf = neghalf4
        P.op("dve", lambda e: e.memset(neghalf4[:], -0.5), writes=[neghalfB])
        psum = [st.enter_context(nc.psum_tensor("ps%d" % i, [128, 512], F32)) for i in range(8)]
        psB = [Buf("ps%d" % i) for i in range(8)]
        x1B = [Buf("x1_%d" % i) for i in range(NBLK)]
        x2B = [Buf("x2_%d" % i) for i in range(NOWN)]
        yrB = [Buf("yr_%d" % i) for i in range(NOWN)]
        qTB = [Buf("qT_%d" % i) for i in range(NOWN)]
        gateB = [Buf("gate%d" % i) for i in range(3)]
        outB = Buf("out")
        wsB = {"w1": [Buf(), Buf()], "w3": [Buf(), Buf()], "w2": [Buf(), Buf()]}
        wsS = P.dma_sem()

        def precast_ffn2_weights():
            items = []
            for nm, src, dst in (("w1", f2w1, w1s), ("w3", f2w3, w3s)):
                for hf in range(2):
                    items.append(("pool", dst[:, hf * 1408:(hf + 1) * 1408], src[:, hf * 1408:(hf + 1) * 1408], [wsB[nm][hf]]))
            for hf in range(2):
                items.append(("pool", w2s[hf * 1408:(hf + 1) * 1408, :], f2w2[hf * 1408:(hf + 1) * 1408, :], [wsB["w2"][hf]]))
            P.dma_group(wsS, items)

        P.op("pool", lambda e: e.memset(ident[:], 0.0), writes=[identB])
        P.op("pool", lambda e: e.affine_select(out=ident[:], in_=ident[:], compare_op=ALU.not_equal,
                                               fill=1.0, base=0, pattern=[[-1, 128]], channel_multiplier=1),
             reads=[identB], writes=[identB])

        if "p0" in phases:
            with ExitStack() as ph:
                cv = sb(ph, "cv", [128, KC]); cvB = Buf()
                cact2 = sb(ph, "cact2", [128, KC, 2]); cact2B = Buf()
                ones = sb(ph, "ones", [128, 128]); onesB = Buf()
                crep = sb(ph, "crep", [128, KC, 128]); crepB = Buf()
                abpp = sb(ph, "abpp", [128, 72]); abppB = Buf()
                ngpp = sb(ph, "ngpp", [128, 3, KC]); ngppB = Buf()
                modpp = sb(ph, "modpp", [128, 6, KC]); modppB = Buf()
                awt = [sb(ph, "awt%d" % i, [128, KC, D]) for i in range(2)]
                awtB = [Buf(), Buf()]
                awtS = [P.dma_sem(), P.dma_sem()]
                abg = sb(ph, "abg", [128, D]); abgB = Buf(); abgS = P.dma_sem()
                gt = sb(ph, "gt", [128, D]); gtB = Buf()
                lv = sb(ph, "lv", [128, 4, 64]); lvB = Buf()
                lp = sb(ph, "lp", [128, 2, 64]); lpB = Buf()
                ls = sb(ph, "ls", [128, 2]); lsB = Buf()
                le = sb(ph, "le", [128, 2]); leB = Buf()
                s0 = P.dma_sem()
                P.dma_group(s0, [("sp", cv[:], cvec, [cvB]), ("sp", abpp[:], ada_b_pp, [abppB]),
                                 ("sp", ngpp[:], norm_g_pp, [ngppB]), ("sp", lv[:], lamv, [lvB])])
                P.op("act", lambda e: e.activation(out=cact2[:, :, 0], in_=cv[:], func=AF.Silu), reads=[cvB], writes=[cact2B])
                P.op("act", lambda e: e.activation(out=cact2[:, :, 1], in_=cv[:], func=AF.Silu), reads=[cvB], writes=[cact2B])
                P.op("dve", lambda e: e.memset(ones[:], 1.0), writes=[onesB])
                for k in range(KC):
                    P.op("dve", lambda e, k=k: e.tensor_scalar(out=crep[:, k, :], in0=ones[:], scalar1=cact2[:, k, 0:1],
                                                               scalar2=None, op0=ALU.mult),
                         reads=[onesB, cact2B], writes=[crepB])
                P.op("dve", lambda e: e.tensor_tensor(out=lp[:, 0, :], in0=lv[:, 0, :], in1=lv[:, 1, :], op=ALU.mult), reads=[lvB], writes=[lpB])
                P.op("dve", lambda e: e.tensor_tensor(out=lp[:, 1, :], in0=lv[:, 2, :], in1=lv[:, 3, :], op=ALU.mult), reads=[lvB], writes=[lpB])
                P.op("dve", lambda e: e.tensor_reduce(out=ls[:], in_=lp[:], axis=AX.X, op=ALU.add), reads=[lpB], writes=[lsB])
                P.op("act", lambda e: e.activation(out=le[:], in_=ls[:], func=AF.Exp), reads=[lsB], writes=[leB])
                P.op("dve", lambda e: e.tensor_tensor(out=neglam[:], in0=le[:, 1:2], in1=le[:, 0:1], op=ALU.subtract), reads=[leB], writes=[neglamB])
                P.op("dve", lambda e: e.tensor_scalar(out=neglam[:], in0=neglam[:], scalar1=-LAMBDA_INIT, scalar2=None, op0=ALU.add),
                     reads=[neglamB], writes=[neglamB])
                ppi = 0
                for j in range(9):
                    slot = j % 2
                    P.dma("sp", awt[slot][:], ada_w[:, j * D:(j + 1) * D].rearrange("(k p) f -> p k f", p=128),
                          writes=[awtB[slot]], sem=awtS[slot])
                    if j % 3 != 2:
                        bank = psum[0]

                        def mm(e, slot=slot, bank=bank):
                            ins = None
                            for dc in range(KC):
                                for k in range(KC):
                                    ins = e.matmul(bank[:, 2 * dc:2 * dc + 2], lhsT=awt[slot][:, k, dc * 128:(dc + 1) * 128],
                                                   rhs=cact2[:, k, :], start=(k == 0), stop=(k == KC - 1))
                            return ins
                        P.op("pe", mm, reads=[awtB[slot], cact2B], writes=[psB[0]])
                        P.op("dve", lambda e, ppi=ppi, j=j, bank=bank: e.tensor_tensor(
                            out=modpp[:, ppi, :], in0=bank[:, 0:16].rearrange("p (d two) -> p d two", two=2)[:, :, 0],
                            in1=abpp[:, j * 8:(j + 1) * 8], op=ALU.add),
                            reads=[abppB], writes=[psB[0], modppB])
                        ppi += 1
                    else:
                        gi = j // 3
                        P.dma("sp", abg[:], ada_b_gate[:, gi, :], writes=[abgB], sem=abgS)
                        for half in range(2):
                            bank = psum[1 + half]

                            def mmg(e, slot=slot, bank=bank, half=half):
                                ins = None
                                for k in range(KC):
                                    ins = e.matmul(bank[:, :], lhsT=crep[:, k, :], rhs=awt[slot][:, k, half * 512:(half + 1) * 512],
                                                   start=(k == 0), stop=(k == KC - 1))
                                return ins
                            P.op("pe", mmg, reads=[awtB[slot], crepB], writes=[psB[1 + half]])
                            P.op("dve", lambda e, bank=bank, half=half: e.tensor_tensor(
                                out=gt[:, half * 512:(half + 1) * 512], in0=bank[:, :], in1=abg[:, half * 512:(half + 1) * 512], op=ALU.add),
                                reads=[abgB], writes=[psB[1 + half], gtB])
                        if gi != 1:
                            P.op("dve", lambda e: e.tensor_scalar(out=gt[:], in0=gt[:], scalar1=0.5, scalar2=None, op0=ALU.mult),
                                 reads=[gtB], writes=[gtB])
                        P.dma("sp", gate_s[gi], gt[:], reads=[gtB], writes=[gateB[gi]], sem=P.dma_sem())
                        if debug:
                            P.dma("sp", dbg["gate"][gi], gt[:], reads=[gtB], writes=[dbgB], sem=P.dma_sem())
                for i in range(3):
                    P.op("dve", lambda e, i=i: e.scalar_tensor_tensor(out=gm[:, i, :], in0=modpp[:, 2 * i + 1, :], scalar=1.0,
                                                                      in1=ngpp[:, i, :], op0=ALU.add, op1=ALU.mult),
                         reads=[modppB, ngppB], writes=[gmB])
                    P.op("dve", lambda e, i=i: e.tensor_copy(out=shv[:, i, :], in_=modpp[:, 2 * i, :]), reads=[modppB], writes=[shB])
                if debug:
                    P.dma("sp", dbg["mod"], modpp[:], reads=[modppB], writes=[dbgB], sem=P.dma_sem())
                P.end_phase()

        def load_w_bf16(dst, dstB, src_pkf, nk, nf, fchunk, sem):
            for f0 in range(0, nf, fchunk):
                f1 = min(nf, f0 + fchunk)
                P.dma("pool", dst[:, :, f0:f1], src_pkf[:, :, f0:f1], writes=[dstB[f0 // fchunk]], sem=sem)

        def norm_part(xslot, xB, xn, xnB, ssq, ssqB, lnexp=False, noact=False):
            if noact:
                P.op("dve", lambda e: e.scalar_tensor_tensor(out=xn[:], in0=xslot, scalar=1.0, in1=xslot, op0=ALU.mult, op1=ALU.mult,
                                                             accum_out=ssq[:, 0:1]),
                     reads=[xB], writes=[xnB, ssqB])
                P.op("dve", lambda e: e.tensor_scalar(out=ssq[:, 1:2], in0=ssq[:, 0:1], scalar1=1.0 / D, scalar2=EPS, op0=ALU.mult, op1=ALU.add),
                     reads=[ssqB], writes=[ssqB])
                P.op("pool", lambda e: e.tensor_tensor(out=ssq[:, 2:3], in0=ssq[:, 1:2], in1=neghalf[:, 0:1], op=ALU.pow),
                     reads=[ssqB, neghalfB], writes=[ssqB])
            else:
                P.op("act", lambda e: e.activation(out=xn[:], in_=xslot, func=AF.Square, accum_out=ssq[:, 0:1]),
                     reads=[xB], writes=[xnB, ssqB])
                if lnexp:
                    P.op("act", lambda e: e.activation(out=ssq[:, 1:2], in_=ssq[:, 0:1], func=AF.Ln, scale=1.0 / D, bias=EPS),
                         reads=[ssqB], writes=[ssqB])
                    P.op("act", lambda e: e.activation(out=ssq[:, 2:3], in_=ssq[:, 1:2], func=AF.Exp, scale=-0.5),
                         reads=[ssqB], writes=[ssqB])
                else:
                    P.op("act", lambda e: e.activation(out=ssq[:, 1:2], in_=ssq[:, 0:1], func=AF.Sqrt, scale=1.0 / D, bias=EPS),
                         reads=[ssqB], writes=[ssqB])
                    P.op("dve", lambda e: e.reciprocal(out=ssq[:, 2:3], in_=ssq[:, 1:2]), reads=[ssqB], writes=[ssqB])
            P.op("pool", lambda e: e.tensor_scalar(out=xn[:], in0=xslot, scalar1=ssq[:, 2:3], scalar2=0.0, op0=ALU.mult, op1=ALU.add),
                 reads=[xB, ssqB], writes=[xnB])

        def tr_round(xn, xnB, i_norm, hT, hTB, col0, tbank, tB, r):
            def tr(e):
                ins = None
                for kk in range(4):
                    k = 4 * r + kk
                    ins = e.transpose(out=tbank[:, kk * 128:(kk + 1) * 128], in_=xn[:, k * 128:(k + 1) * 128], identity=ident[:])
                return ins
            P.op("pe", tr, reads=[xnB, identB], writes=[tB])
            for kk in range(4):
                k = 4 * r + kk
                if r % 2 == 0:
                    P.op("dve", lambda e, k=k, kk=kk: e.tensor_scalar(
                        out=hT[:, k, col0:col0 + 128], in0=tbank[:, kk * 128:(kk + 1) * 128],
                        scalar1=gm[:, i_norm, k:k + 1], scalar2=shv[:, i_norm, k:k + 1], op0=ALU.mult, op1=ALU.add),
                        reads=[gmB, shB], writes=[tB, hTB[k]])
                else:
                    P.op("act", lambda e, k=k, kk=kk: e.activation(
                        out=hT[:, k, col0:col0 + 128], in_=tbank[:, kk * 128:(kk + 1) * 128], func=AF.Identity,
                        scale=gm[:, i_norm, k:k + 1], bias=shv[:, i_norm, k:k + 1]),
                        reads=[gmB, shB], writes=[tB, hTB[k]])

        def tr_part(xn, xnB, i_norm, hT, hTB, col0, tbanks, tBs):
            for r in range(2):
                tbank, tB = tbanks[r], tBs[r]

                def tr(e, r=r, tbank=tbank):
                    ins = None
                    for kk in range(4):
                        k = 4 * r + kk
                        ins = e.transpose(out=tbank[:, kk * 128:(kk + 1) * 128], in_=xn[:, k * 128:(k + 1) * 128], identity=ident[:])
                    return ins
                P.op("pe", tr, reads=[xnB, identB], writes=[tB])
                for kk in range(4):
                    k = 4 * r + kk
                    if r % 2 == 0:
                        P.op("dve", lambda e, k=k, kk=kk, tbank=tbank: e.tensor_scalar(
                            out=hT[:, k, col0:col0 + 128], in0=tbank[:, kk * 128:(kk + 1) * 128],
                            scalar1=gm[:, i_norm, k:k + 1], scalar2=shv[:, i_norm, k:k + 1], op0=ALU.mult, op1=ALU.add),
                            reads=[gmB, shB], writes=[tB, hTB[k]])
                    else:
                        P.op("act", lambda e, k=k, kk=kk, tbank=tbank: e.activation(
                            out=hT[:, k, col0:col0 + 128], in_=tbank[:, kk * 128:(kk + 1) * 128], func=AF.Identity,
                            scale=gm[:, i_norm, k:k + 1], bias=shv[:, i_norm, k:k + 1]),
                            reads=[gmB, shB], writes=[tB, hTB[k]])

        def ffn_phase(tag, n_groups, src_rows, srcB, dst_rows, dstB, w1, w3, w2, i_norm, gate_idx, final, dbg_rows=None, pre=None):
            TG = 2
            TT = TG * 128
            n_tiles = n_groups // TG
            with ExitStack() as ph:
                w1b = sb(ph, tag + "w1b", [128, KC, DFF], BF16)
                w3b = sb(ph, tag + "w3b", [128, KC, DFF], BF16)
                w2b = sb(ph, tag + "w2b", [128, NF, D], BF16)
                FCH = 704
                w1B = [Buf() for _ in range(4)]; w3B = [Buf() for _ in range(4)]
                w2B = [Buf() for _ in range(2)]
                gb = sb(ph, tag + "gb", [128, D]); gbB = Buf()
                NSLOT = 6
                xr = [sb(ph, tag + "xr%d" % i, [128, D]) for i in range(NSLOT)]
                xrB = [Buf() for _ in range(NSLOT)]
                xrS = [P.dma_sem() for _ in range(NSLOT)]
                stS = xrS
                xn = [sb(ph, tag + "xn%d" % i, [128, D]) for i in range(2)]
                xnB = [Buf(), Buf()]
                ssq = [sb(ph, tag + "ssq%d" % i, [128, 4]) for i in range(2)]
                ssqB = [Buf(), Buf()]
                hT = sb(ph, tag + "hT", [128, KC, TT], BF16); hTB = [Buf() for _ in range(KC)]
                uT = sb(ph, tag + "uT", [128, NF, TT], BF16); uTB = [Buf() for _ in range(NF)]
                sl = [sb(ph, tag + "sl%d" % i, [128, TT]) for i in range(2)]
                slB = [Buf(), Buf()]
                tmp = [sb(ph, tag + "tmp%d" % i, [128, 512]) for i in range(2)]
                tmpB = [Buf(), Buf()]
                if final:
                    fg = sb(ph, tag + "fg", [128, D]); fgB = Buf()
                    fjunk = sb(ph, tag + "fjunk", [128, D]); fjunkB = Buf()
                    fsq = [sb(ph, tag + "fsq%d" % i, [128, 4]) for i in range(2)]; fsqB = [Buf(), Buf()]
                    P.dma("sp", fg[:], final_g_b, writes=[fgB], sem=P.dma_sem())
                w1v = w1.rearrange("(k p) f -> p k f", p=128)
                w3v = w3.rearrange("(k p) f -> p k f", p=128)
                w2v = w2.rearrange("(f p) d -> p f d", p=128)
                def load_weights():
                  if pre is None:
                    for q in range(4):
                        P.dma_group(P.dma_sem(), [("pool", w1b[:, :, q * FCH:(q + 1) * FCH], w1v[:, :, q * FCH:(q + 1) * FCH], [w1B[q]]),
                                                  ("pool", w3b[:, :, q * FCH:(q + 1) * FCH], w3v[:, :, q * FCH:(q + 1) * FCH], [w3B[q]])])
                    P.dma_group(P.dma_sem(), [("pool", w2b[:, q * 11:(q + 1) * 11, :], w2v[:, q * 11:(q + 1) * 11, :], [w2B[q]]) for q in range(2)])
                  else:
                    p1, p3, p2, pB = pre
                    p1v = p1.rearrange("(k p) f -> p k f", p=128)
                    p3v = p3.rearrange("(k p) f -> p k f", p=128)
                    p2v = p2.rearrange("(f p) d -> p f d", p=128)
                    for q in range(4):
                        P.dma_group(P.dma_sem(), [("sp", w1b[:, :, q * FCH:(q + 1) * FCH], p1v[:, :, q * FCH:(q + 1) * FCH], [w1B[q]], pB["w1"]),
                                                  ("sp", w3b[:, :, q * FCH:(q + 1) * FCH], p3v[:, :, q * FCH:(q + 1) * FCH], [w3B[q]], pB["w3"])])
                    P.dma_group(P.dma_sem(), [("sp", w2b[:, q * 11:(q + 1) * 11, :], p2v[:, q * 11:(q + 1) * 11, :], [w2B[q]], pB["w2"])
                                              for q in range(2)])
                P.dma("sp", gb[:], gate_s[gate_idx], reads=[gateB[gate_idx]], writes=[gbB], sem=P.dma_sem())
                A = [psum[0], psum[1]]; AB = [psB[0], psB[1]]
                Bk = [psum[2], psum[3]]; BB = [psB[2], psB[3]]
                Dk = [psum[4], psum[5]]; DB = [psB[4], psB[5]]
                Tk = [psum[6], psum[7]]; TB = [psB[6], psB[7]]
                cnt = {"g": 0, "f": 0, "d": 0}

                def stage_N(t):
                    for j in range(TG):
                        gidx = t * TG + j
                        slot = gidx % NSLOT
                        P.dma("sp", xr[slot][:], src_rows(gidx), reads=[srcB(gidx)], writes=[xrB[slot]], sem=xrS[slot])
                        norm_part(xr[slot][:], xrB[slot], xn[j], xnB[j], ssq[j], ssqB[j], noact=True)

                def x_rounds(t):
                    return [(lambda j=j, r=r: tr_round(xn[j], xnB[j], i_norm, hT, hTB, j * 128, Tk[r], TB[r], r))
                            for j in range(TG) for r in range(2)]

                def stage_X(t):
                    for f_ in x_rounds(t):
                        f_()

                def stage_U(t):
                    for f in range(NF):
                        if f == 3 and t + 1 < n_tiles:
                            stage_N(t + 1)
                        n = cnt["f"] % 2
                        cnt["f"] += 1
                        q = (f * 128) // FCH
                        q2 = (f * 128 + 127) // FCH
                        wr1 = [w1B[q]] + ([w1B[q2]] if q2 != q else [])
                        wr3 = [w3B[q]] + ([w3B[q2]] if q2 != q else [])

                        def mma(e, f=f, n=n):
                            ins = None
                            for k in range(KC):
                                ins = e.matmul(A[n][:, 0:TT], lhsT=w1b[:, k, f * 128:(f + 1) * 128], rhs=hT[:, k, :],
                                               start=(k == 0), stop=(k == KC - 1))
                            return ins

                        def mmb(e, f=f, n=n):
                            ins = None
                            for k in range(KC):
                                ins = e.matmul(Bk[n][:, 0:TT], lhsT=w3b[:, k, f * 128:(f + 1) * 128], rhs=hT[:, k, :],
                                               start=(k == 0), stop=(k == KC - 1))
                            return ins
                        P.op("pe", mma, reads=wr1 + hTB, writes=[AB[n]])
                        P.op("pe", mmb, reads=wr3 + hTB, writes=[BB[n]])
                        P.op("act", lambda e, n=n: e.activation(out=sl[n][:], in_=A[n][:, 0:TT], func=AF.Silu),
                             writes=[AB[n], slB[n]])
                        P.op("dve", lambda e, n=n, f=f: e.tensor_tensor(out=uT[:, f, :], in0=sl[n][:], in1=Bk[n][:, 0:TT], op=ALU.mult),
                             reads=[slB[n]], writes=[BB[n], uTB[f]])

                def stage_D(t, inter=()):
                    inter = list(inter)
                    for j in range(TG):
                        gidx = t * TG + j
                        slot = gidx % NSLOT
                        for half in range(2):
                            if inter:
                                inter.pop(0)()
                            n = cnt["d"] % 2
                            cnt["d"] += 1

                            def mmd(e, j=j, half=half, n=n):
                                ins = None
                                for f in range(NF):
                                    ins = e.matmul(Dk[n][:, :], lhsT=uT[:, f, j * 128:(j + 1) * 128],
                                                   rhs=w2b[:, f, half * 512:(half + 1) * 512], start=(f == 0), stop=(f == NF - 1))
                                return ins
                            P.op("pe", mmd, reads=uTB + w2B, writes=[DB[n]])
                            P.op("dve", lambda e, n=n, half=half: e.tensor_tensor(
                                out=tmp[n][:], in0=Dk[n][:, :], in1=gb[:, half * 512:(half + 1) * 512], op=ALU.mult),
                                reads=[gbB], writes=[DB[n], tmpB[n]])
                            P.op("pool", lambda e, n=n, half=half, slot=slot: e.tensor_tensor(
                                out=xr[slot][:, half * 512:(half + 1) * 512], in0=xr[slot][:, half * 512:(half + 1) * 512],
                                in1=tmp[n][:], op=ALU.add),
                                reads=[tmpB[n]], writes=[xrB[slot]])
                        if final:
                            n2 = cnt["g"] % 2
                            cnt["g"] += 1
                            xnf, xnfB, sq, sqB = fjunk, fjunkB, fsq[n2], fsqB[n2]
                            P.op("dve", lambda e, slot=slot, xnf=xnf, sq=sq: e.scalar_tensor_tensor(
                                out=xnf[:], in0=xr[slot][:], scalar=1.0, in1=xr[slot][:], op0=ALU.mult, op1=ALU.mult, accum_out=sq[:, 0:1]),
                                reads=[xrB[slot]], writes=[xnfB, sqB])
                            P.op("dve", lambda e, sq=sq: e.tensor_scalar(out=sq[:, 1:2], in0=sq[:, 0:1], scalar1=1.0 / D, scalar2=EPS,
                                                                         op0=ALU.mult, op1=ALU.add), reads=[sqB], writes=[sqB])
                            P.op("pool", lambda e, sq=sq: e.tensor_tensor(out=sq[:, 2:3], in0=sq[:, 1:2], in1=neghalf[:, 0:1], op=ALU.pow),
                                 reads=[sqB, neghalfB], writes=[sqB])
                            P.op("dve", lambda e, slot=slot, sq=sq: e.scalar_tensor_tensor(
                                out=xr[slot][:], in0=xr[slot][:], scalar=sq[:, 2:3], in1=fg[:], op0=ALU.mult, op1=ALU.mult),
                                reads=[sqB, fgB], writes=[xrB[slot]])
                        P.dma("sp", dst_rows(gidx), xr[slot][:], reads=[xrB[slot]], writes=[dstB(gidx)], sem=stS[slot])
                        if dbg_rows is not None:
                            P.dma("sp", dbg_rows(gidx), xr[slot][:], reads=[xrB[slot]], writes=[dbgB], sem=stS[slot])

                if pre is None:
                    load_weights()
                    stage_N(0)
                else:
                    stage_N(0)
                    load_weights()
                stage_X(0)
                for t in range(n_tiles):
                    stage_U(t)
                    stage_D(t, x_rounds(t + 1) if t + 1 < n_tiles else ())
                P.end_phase()

        if "p1" in phases:
            xsB = Buf("xs")
            ffn_phase("a", NBLK, lambda g: xs[g * 128:(g + 1) * 128, :], lambda g: xsB,
                      lambda g: x1_s[g * 128:(g + 1) * 128, :], lambda g: x1B[g],
                      f1w1, f1w3, f1w2, 0, 0, False,
                      dbg_rows=(lambda g: dbg["x1"][g * 128:(g + 1) * 128, :]) if debug else None)

        NT = NBLK // 4
        if "p2a" in phases:
            with ExitStack() as ph:
                wrb = sb(ph, "wrb", [128, KC, 1024], BF16); wrB = [Buf(), Buf()]
                NSLOT = 6
                xr = [sb(ph, "rxr%d" % i, [128, D]) for i in range(NSLOT)]
                xrB = [Buf() for _ in range(NSLOT)]
                xrS = [P.dma_sem() for _ in range(NSLOT)]
                xn = [sb(ph, "rxn%d" % i, [128, D]) for i in range(4)]; xnB = [Buf() for _ in range(4)]
                ssq = [sb(ph, "rssq%d" % i, [128, 4]) for i in range(4)]; ssqB = [Buf() for _ in range(4)]
                hT2 = [sb(ph, "rhT%d" % i, [128, KC, 512], BF16) for i in range(2)]; hT2B = [[Buf() for _ in range(KC)] for _ in range(2)]
                def dbl(name, shape):
                    return [sb(ph, name + "%d" % i, shape) for i in range(2)], [[Buf() for _ in range(4)] for _ in range(2)]
                xrt, xrtB = dbl("xrt", [128, 4, 515])
                cvt, cvtB = dbl("cvt", [128, 4, 512])
                rt, rtB = dbl("rt", [128, 4, 512])
                it, itB = dbl("it", [128, 4, 512])
                grs, grsB = dbl("grs", [128, 4, 256])
                gz, gzB = dbl("gz", [128, 4, 256])
                at = sb(ph, "at", [128, 4, 512]); atB = [Buf() for _ in range(4)]
                qt_ = sb(ph, "sqt", [128, 4, 512]); qtB_ = [Buf() for _ in range(4)]
                hh = sb(ph, "hh", [128, 4, 512]); hhB = [Buf() for _ in range(4)]
                hprev = sb(ph, "hprev", [128, 4]); hprevB = [Buf() for _ in range(4)]
                yrt = [sb(ph, "yrt%d" % i, [128, 4, 256], BF16) for i in range(2)]
                yrtB = [Buf(), Buf()]; yrtS = [P.dma_sem(), P.dma_sem()]
                cwp = sb(ph, "cwp", [128, 4, 4]); cwpB = Buf()
                rpp = sb(ph, "rpp", [128, 4, 4]); rppB = Buf()
                c8 = sb(ph, "c8", [128, 4, 2]); c8B = Buf()
                gab = sb(ph, "gab", [128, 4, 128]); gabB = Buf()
                gib = sb(ph, "gib", [128, 4, 128]); gibB = Buf()
                m0t = sb(ph, "m0t", [128, 128]); m0B = Buf()
                s0 = P.dma_sem()
                wv = w_in.rearrange("(k p) f -> p k f", p=128)
                ws_ = P.dma_sem()
                P.dma_group(ws_, [("pool", wrb[:, :, q * 512:(q + 1) * 512], wv[:, :, 1536 + q * 512:1536 + (q + 1) * 512], [wrB[q]])
                                  for q in range(2)])
                P.dma_group(s0, [("sp", cwp[:], conv_w_pp, [cwpB]), ("sp", rpp[:], rnn_pp, [rppB]), ("sp", gab[:], ga_bd, [gabB]),
                                 ("sp", gib[:], gi_bd, [gibB]), ("sp", m0t[:], m0, [m0B])])
                precast_ffn2_weights()
                P.op("act", lambda e: e.activation(out=c8[:, :, 0], in_=rpp[:, :, 3], func=AF.Exp, scale=-1.0), reads=[rppB], writes=[c8B])
                P.op("act", lambda e: e.activation(out=c8[:, :, 0], in_=c8[:, :, 0], func=AF.Ln, bias=1.0), reads=[c8B], writes=[c8B])
                P.op("dve", lambda e: e.tensor_scalar(out=c8[:, :, 1], in0=c8[:, :, 0], scalar1=-16.0, scalar2=None, op0=ALU.mult), reads=[c8B], writes=[c8B])
                P.op("dve", lambda e: e.tensor_scalar(out=c8[:, :, 0], in0=c8[:, :, 0], scalar1=-8.0, scalar2=None, op0=ALU.mult), reads=[c8B], writes=[c8B])
                dg = sb(ph, "dg", [128, 4, 4, 128]); dgB = Buf()
                for c in range(4):
                    for jt in range(4):
                        P.op("dve", lambda e, c=c, jt=jt: e.tensor_scalar(out=dg[:, c, jt, :], in0=ident[:], scalar1=cwp[:, c, jt:jt + 1],
                                                                          scalar2=None, op0=ALU.mult),
                             reads=[identB, cwpB], writes=[dgB])
                for c in range(4):
                    P.op("dve", lambda e, c=c: e.memset(xrt[0][:, c, 0:3], 0.0), writes=[xrtB[0][c]])
                    P.op("dve", lambda e, c=c: e.memset(hprev[:, c:c + 1], 0.0), writes=[hprevB[c]])

                def A_N(t):
                    for j in range(4):
                        gidx = t * 4 + j
                        slot = gidx % NSLOT
                        P.dma("sp", xr[slot][:], x1_s[gidx * 128:(gidx + 1) * 128, :], reads=[x1B[gidx]], writes=[xrB[slot]], sem=xrS[slot])
                        norm_part(xr[slot][:], xrB[slot], xn[j], xnB[j], ssq[j], ssqB[j], lnexp=True)

                def A_X(t):
                    for j in range(4):
                        tr_part(xn[j], xnB[j], 1, hT2[t % 2], hT2B[t % 2], j * 128, [psum[6], psum[7]], [psB[6], psB[7]])

                def A_T(t):
                    A_N(t)
                    A_X(t)

                def A_proj(t):
                    p = t % 2
                    hT_, hTB_ = hT2[p], hT2B[p]
                    for c in range(4):
                        bx, bxB = psum[c], psB[c]
                        bg, bgB = psum[4 + c % 2], psB[4 + c % 2]

                        def mmx(e, c=c, bx=bx):
                            ins = None
                            for k in range(KC):
                                ins = e.matmul(bx[:, :], lhsT=wrb[:, k, c * 128:(c + 1) * 128], rhs=hT_[:, k, :], start=(k == 0), stop=(k == KC - 1))
                            return ins
                        P.op("pe", mmx, reads=[wrB[0]] + hTB_, writes=[bxB])

                        def mmg(e, c=c, bg=bg):
                            ins = None
                            for jj in range(2):
                                for k in range(KC):
                                    ins = e.matmul(bg[:, jj * 128:(jj + 1) * 128], lhsT=wrb[:, k, 512 + c * 128:512 + (c + 1) * 128],
                                                   rhs=hT_[:, k, (2 * jj + 1) * 128:(2 * jj + 2) * 128], start=(k == 0), stop=(k == KC - 1))
                            return ins
                        P.op("pe", mmg, reads=[wrB[1]] + hTB_, writes=[bgB])
                        P.op("act", lambda e, c=c, bx=bx: e.copy(out=xrt[p][:, c, 3:515], in_=bx[:, :]), writes=[bxB, xrtB[p][c]])
                        P.op("act", lambda e, c=c, bg=bg: e.copy(out=grs[p][:, c, :], in_=bg[:, 0:256]), writes=[bgB, grsB[p][c]])
                        if t == 0:
                            P.op("dve", lambda e, c=c: e.tensor_tensor(out=xrt[p][:, c, 3:131], in0=xrt[p][:, c, 3:131], in1=m0t[:], op=ALU.mult),
                                 reads=[m0B], writes=[xrtB[p][c]])
                        P.op("dve", lambda e, c=c: e.tensor_copy(out=xrt[1 - p][:, c, 0:3], in_=xrt[p][:, c, 512:515]),
                             reads=[xrtB[p][c]], writes=[xrtB[1 - p][c]])

                def B_conv(t):
                    p = t % 2
                    for c in range(4):
                        def mmc(e, c=c):
                            ins = None
                            for jt in range(4):
                                ins = e.matmul(psum[c][:, :], lhsT=dg[:, c, jt, :], rhs=xrt[p][:, c, jt:jt + 512],
                                               start=(jt == 0), stop=(jt == 3))
                            return ins
                        P.op("pe", mmc, reads=[xrtB[p][c], dgB], writes=[psB[c]])
                        P.op("act", lambda e, c=c: e.activation(out=cvt[p][:, c, :], in_=psum[c][:, :], func=AF.Identity,
                                                                bias=rpp[:, c, 0:1]),
                             reads=[rppB], writes=[psB[c], cvtB[p][c]])

                def B_gates(t):
                    p = t % 2
                    for c in range(4):
                        bx, bxB = psum[c], psB[c]
                        bg, bgB = psum[4 + c % 2], psB[4 + c % 2]
                        P.op("pe", lambda e, c=c, bx=bx: e.matmul(bx[:, :], lhsT=gab[:, c, :], rhs=cvt[p][:, c, :], start=True, stop=True),
                             reads=[gabB, cvtB[p][c]], writes=[bxB])
                        P.op("pe", lambda e, c=c, bg=bg: e.matmul(bg[:, :], lhsT=gib[:, c, :], rhs=cvt[p][:, c, :], start=True, stop=True),
                             reads=[gibB, cvtB[p][c]], writes=[bgB])
                        P.op("dve", lambda e, c=c: e.tensor_tensor(out=gz[p][:, c, :], in0=grs[p][:, c, :], in1=grs[p][:, c, :], op=ALU.mult),
                             reads=[grsB[p][c]], writes=[gzB[p][c]])
                        P.op("dve", lambda e, c=c: e.tensor_scalar(out=gz[p][:, c, :], in0=gz[p][:, c, :], scalar1=0.044715, scalar2=1.0,
                                                                   op0=ALU.mult, op1=ALU.add), writes=[gzB[p][c]])
                        P.op("dve", lambda e, c=c: e.tensor_tensor(out=gz[p][:, c, :], in0=gz[p][:, c, :], in1=grs[p][:, c, :], op=ALU.mult),
                             reads=[grsB[p][c]], writes=[gzB[p][c]])
                        P.op("act", lambda e, c=c, bx=bx: e.activation(out=rt[p][:, c, :], in_=bx[:, :], func=AF.Sigmoid, bias=rpp[:, c, 1:2]),
                             reads=[rppB], writes=[bxB, rtB[p][c]])
                        P.op("act", lambda e, c=c, bg=bg: e.activation(out=it[p][:, c, :], in_=bg[:, :], func=AF.Sigmoid, bias=rpp[:, c, 2:3]),
                             reads=[rppB], writes=[bgB, itB[p][c]])
                        P.op("act", lambda e, c=c: e.activation(out=gz[p][:, c, :], in_=gz[p][:, c, :], func=AF.Sigmoid, scale=1.5957691216057308),
                             writes=[gzB[p][c]])
                        P.op("pool", lambda e, c=c: e.tensor_tensor(out=gz[p][:, c, :], in0=gz[p][:, c, :], in1=grs[p][:, c, :], op=ALU.mult),
                             reads=[grsB[p][c]], writes=[gzB[p][c]])

                def C_act(t):
                    p = t % 2
                    for c in range(4):
                        P.op("act", lambda e, c=c: e.activation(out=at[:, c, :], in_=rt[p][:, c, :], func=AF.Exp, scale=c8[:, c, 0:1]),
                             reads=[rtB[p][c], c8B], writes=[atB[c]])
                        P.op("act", lambda e, c=c: e.activation(out=qt_[:, c, :], in_=rt[p][:, c, :], func=AF.Exp, scale=c8[:, c, 1:2]),
                             reads=[rtB[p][c], c8B], writes=[qtB_[c]])
                        P.op("pool", lambda e, c=c: e.tensor_tensor(out=it[p][:, c, :], in0=it[p][:, c, :], in1=cvt[p][:, c, :], op=ALU.mult),
                             reads=[cvtB[p][c]], writes=[itB[p][c]])
                    for c in range(4):
                        P.op("act", lambda e, c=c: e.activation(out=qt_[:, c, :], in_=qt_[:, c, :], func=AF.Ln, scale=-1.0, bias=1.0),
                             writes=[qtB_[c]])
                        P.op("act", lambda e, c=c: e.activation(out=qt_[:, c, :], in_=qt_[:, c, :], func=AF.Exp, scale=0.5),
                             writes=[qtB_[c]])

                def C_dve(t):
                    p = t % 2
                    yslot = t % 2
                    for c in range(4):
                        P.op("pool", lambda e, c=c: e.tensor_tensor(out=it[p][:, c, :], in0=it[p][:, c, :], in1=qt_[:, c, :], op=ALU.mult),
                             reads=[qtB_[c]], writes=[itB[p][c]])
                        if t == 0:
                            P.op("pool", lambda e, c=c: e.tensor_tensor(out=it[p][:, c, 0:128], in0=it[p][:, c, 0:128], in1=m0t[:], op=ALU.mult),
                                 reads=[m0B], writes=[itB[p][c]])
                    for c in range(4):
                        P.op("dve", lambda e, c=c: e.tensor_tensor_scan(out=hh[:, c, :], data0=at[:, c, :], data1=it[p][:, c, :],
                                                                        initial=hprev[:, c:c + 1], op0=ALU.mult, op1=ALU.add),
                             reads=[atB[c], itB[p][c], hprevB[c]], writes=[hhB[c]])
                        P.op("dve", lambda e, c=c: e.tensor_copy(out=hprev[:, c:c + 1], in_=hh[:, c, 511:512]), reads=[hhB[c]], writes=[hprevB[c]])
                    for c in range(4):
                        for jj in range(2):
                            P.op("pool", lambda e, c=c, jj=jj: e.tensor_tensor(
                                out=yrt[yslot][:, c, jj * 128:(jj + 1) * 128], in0=hh[:, c, (2 * jj + 1) * 128:(2 * jj + 2) * 128],
                                in1=gz[p][:, c, jj * 128:(jj + 1) * 128], op=ALU.mult),
                                reads=[hhB[c], gzB[p][c]], writes=[yrtB[yslot]])
                    P.dma("sp", yr_s.rearrange("(c p) n -> p c n", p=128)[:, :, t * 256:(t + 1) * 256], yrt[yslot][:],
                          reads=[yrtB[yslot]], writes=[yrB[2 * t], yrB[2 * t + 1]], sem=yrtS[yslot])

                A_T(0)
                A_proj(0)
                for i in range(NT + 1):
                    if i >= 1:
                        C_act(i - 1)
                    if i < NT:
                        B_conv(i)
                    if i + 1 < NT:
                        A_N(i + 1)
                        A_X(i + 1)
                    if i < NT:
                        B_gates(i)
                    if i + 1 < NT:
                        A_proj(i + 1)
                    if i >= 1:
                        C_dve(i - 1)
                P.end_phase()

        if "p2b" in phases or "p3" in phases:
            with ExitStack() as kv:
                KT = sb(kv, "KT", [128, 4, S], BF16)
                KTB = [Buf() for _ in range(NT)]
                V = sb(kv, "V", [128, NBLK, 4, 129], BF16)
                VB = [Buf() for _ in range(NBLK)]
                VoneB = Buf()
                P.op("pool", lambda e: e.memset(V[:, :, :, 128:129], 1.0), writes=[VoneB])

                if "p2b" in phases:
                    with ExitStack() as ph:
                        wqb = sb(ph, "wqb", [128, KC, 1536], BF16); wqB = [Buf() for _ in range(3)]
                        NSLOT = 4
                        xr = [sb(ph, "kxr%d" % i, [128, D]) for i in range(NSLOT)]
                        xrB = [Buf() for _ in range(NSLOT)]
                        xrS = [P.dma_sem() for _ in range(NSLOT)]
                        xn = [sb(ph, "kxn%d" % i, [128, D]) for i in range(4)]; xnB = [Buf() for _ in range(4)]
                        ssq = [sb(ph, "kssq%d" % i, [128, 4]) for i in range(4)]; ssqB = [Buf() for _ in range(4)]
                        hT2 = [sb(ph, "khT%d" % i, [128, KC, 512], BF16) for i in range(2)]; hT2B = [[Buf() for _ in range(KC)] for _ in range(2)]
                        qst = [sb(ph, "qst%d" % i, [128, 4, 256], BF16) for i in range(2)]
                        qstB = [Buf(), Buf()]; qstS = [P.dma_sem(), P.dma_sem()]
                        wv = w_in.rearrange("(k p) f -> p k f", p=128)
                        ws_ = P.dma_sem()
                        P.dma_group(ws_, [("pool", wqb[:, :, q * 512:(q + 1) * 512], wv[:, :, q * 512:(q + 1) * 512], [wqB[q]])
                                          for q in range(3)])
                        gcnt = [0]
                        pc = 0

                        def kst_N(t):
                            for j in range(4):
                                gidx = t * 4 + j
                                slot = gidx % NSLOT
                                P.dma("sp", xr[slot][:], x1_s[gidx * 128:(gidx + 1) * 128, :], reads=[x1B[gidx]], writes=[xrB[slot]], sem=xrS[slot])
                                norm_part(xr[slot][:], xrB[slot], xn[j], xnB[j], ssq[j], ssqB[j])

                        def kst_X(t):
                            for j in range(4):
                                tr_part(xn[j], xnB[j], 1, hT2[t % 2], hT2B[t % 2], j * 128, [psum[6], psum[7]], [psB[6], psB[7]])

                        kst_N(0)
                        kst_X(0)
                        for t in range(NT):
                            if t + 1 < NT:
                                kst_N(t + 1)
                            hT, hTB = hT2[t % 2], hT2B[t % 2]
                            qs = t % 2
                            for c in range(4):
                                b = pc % 3
                                pc += 1

                                def mmk(e, c=c, b=b, hT=hT):
                                    ins = None
                                    for k in range(KC):
                                        ins = e.matmul(psum[b][:, :], lhsT=wqb[:, k, 512 + c * 128:512 + (c + 1) * 128], rhs=hT[:, k, :],
                                                       start=(k == 0), stop=(k == KC - 1))
                                    return ins
                                P.op("pe", mmk, reads=[wqB[1]] + hTB, writes=[psB[b]])
                                P.op("act", lambda e, c=c, b=b, t=t: e.copy(out=KT[:, c, t * 512:(t + 1) * 512], in_=psum[b][:, :]),
                                     writes=[psB[b], KTB[t]])
                                b2 = 3 + (pc % 3)

                                def mmq(e, c=c, b2=b2, hT=hT):
                                    ins = None
                                    for jj in range(2):
                                        for k in range(KC):
                                            ins = e.matmul(psum[b2][:, jj * 128:(jj + 1) * 128], lhsT=wqb[:, k, c * 128:(c + 1) * 128],
                                                           rhs=hT[:, k, (2 * jj + 1) * 128:(2 * jj + 2) * 128], start=(k == 0), stop=(k == KC - 1))
                                    return ins
                                P.op("pe", mmq, reads=[wqB[0]] + hTB, writes=[psB[b2]])
                                P.op("dve", lambda e, c=c, b2=b2, qs=qs: e.tensor_copy(out=qst[qs][:, c, :], in_=psum[b2][:, 0:256]),
                                     writes=[psB[b2], qstB[qs]])
                            for j in range(4):
                                b = pc % 3
                                pc += 1
                                blk = t * 4 + j

                                def mmv(e, j=j, b=b, hT=hT):
                                    ins = None
                                    for k in range(KC):
                                        ins = e.matmul(psum[b][:, :], lhsT=hT[:, k, j * 128:(j + 1) * 128], rhs=wqb[:, k, 1024:1536],
                                                       start=(k == 0), stop=(k == KC - 1))
                                    return ins
                                P.op("pe", mmv, reads=[wqB[2]] + hTB, writes=[psB[b]])
                                P.op("act" if j % 2 == 0 else "dve",
                                     (lambda e, b=b, blk=blk: e.copy(out=V[:, blk, :, 0:128], in_=psum[b][:, :].rearrange("p (c v) -> p c v", c=4)))
                                     if j % 2 == 0 else
                                     (lambda e, b=b, blk=blk: e.tensor_copy(out=V[:, blk, :, 0:128], in_=psum[b][:, :].rearrange("p (c v) -> p c v", c=4))),
                                     writes=[psB[b], VB[blk]])
                            if t + 1 < NT:
                                kst_X(t + 1)
                            P.dma("sp", qT_s.rearrange("(c p) n -> p c n", p=128)[:, :, t * 256:(t + 1) * 256], qst[qs][:],
                                  reads=[qstB[qs]], writes=[qTB[2 * t], qTB[2 * t + 1]], sem=qstS[qs])
                            if debug:
                                P.dma("sp", dbg["qT"].rearrange("(c p) n -> p c n", p=128)[:, :, t * 256:(t + 1) * 256], qst[qs][:],
                                      reads=[qstB[qs]], writes=[dbgB], sem=qstS[qs])
                        if debug:
                            P.dma("sp", dbg["kT"], KT[:], reads=KTB, writes=[dbgB], sem=P.dma_sem())
                            P.dma("sp", dbg["v"], V[:], reads=VB + [VoneB], writes=[dbgB], sem=P.dma_sem())
                        P.end_phase()

                if "p3" in phases:
                    with ExitStack() as ph:
                        wob = sb(ph, "wob", [128, KC, D], BF16); woB = [Buf(), Buf()]
                        g2b = sb(ph, "g2b", [128, D]); g2B = Buf()
                        bd = sb(ph, "bd", [128, 8, 128]); bdB = Buf()
                        bp = sb(ph, "bp", [128, 8, 128]); bpB = Buf()
                        bp0 = sb(ph, "bp0", [128, 8, 128]); bp0B = Buf()
                        cmt = sb(ph, "cmt", [128, 8]); cmB = Buf()
                        col0 = sb(ph, "col0", [128, 1]); col0B = Buf()
                        subg = sb(ph, "subg", [128, 128]); subgB = Buf()
                        qz = [[sb(ph, "qz%d_%d" % (i, e_), [128, 4, 512], BF16) for e_ in range(2)] for i in range(1)]
                        qzB = [Buf()]; qzS = [P.dma_sem()]
                        NPT = 5
                        pt = [sb(ph, "pt%d" % i, [128, 512], BF16) for i in range(NPT)]
                        ptB = [Buf() for _ in range(NPT)]
                        stmp = [sb(ph, "stmp%d" % i, [128, 128]) for i in range(2)]
                        stmpB = [Buf(), Buf()]
                        om = [sb(ph, "om%d" % i, [128, 4, 128]) for i in range(2)]
                        omB = [Buf(), Buf()]
                        rc = sb(ph, "rc", [128, 8]); rcB = Buf()
                        oh = sb(ph, "oh", [128, 4, 128]); ohB = Buf()
                        osq = sb(ph, "osq", [128, 4, 128]); osqB = Buf()
                        oss = sb(ph, "oss", [128, 12]); ossB = Buf()
                        oall = sb(ph, "oall", [128, 4, 512]); oallB = [Buf() for _ in range(4)]
                        oT = sb(ph, "oT", [128, 4, 128], BF16); oTB = Buf()
                        yT = [sb(ph, "yT%d" % i, [128, 4, 128], BF16) for i in range(2)]
                        yTB = [Buf(), Buf()]; yTS = [P.dma_sem(), P.dma_sem()]
                        x1t = [sb(ph, "x1t%d" % i, [128, D]) for i in range(2)]
                        x1tB = [Buf(), Buf()]; x1tS = [P.dma_sem(), P.dma_sem()]; x1oS = x1tS
                        mtmp = [sb(ph, "mtmp%d" % i, [128, 512]) for i in range(2)]
                        mtmpB = [Buf(), Buf()]
                        s0 = P.dma_sem()
                        ws_ = P.dma_sem()
                        wov = w_out.rearrange("(k p) f -> p k f", p=128)
                        P.dma_group(ws_, [("pool", wob[:, :, q * 512:(q + 1) * 512], wov[:, :, q * 512:(q + 1) * 512], [woB[q]])
                                          for q in range(2)])
                        P.dma("sp", g2b[:], gate_s[1], reads=[gateB[1]], writes=[g2B], sem=P.dma_sem())
                        P.dma_group(s0, [("sp", bd[:], bdm, [bdB]), ("sp", bp[:], bpm, [bpB]), ("sp", bp0[:], bp0m, [bp0B]),
                                         ("sp", cmt[:], cm_in, [cmB]), ("sp", col0[:], col0_in, [col0B]), ("sp", subg[:], subg_b, [subgB])])
                        for m in range(8):
                            for t_, tB_ in ((bd, bdB), (bp, bpB), (bp0, bp0B)):
                                P.op("dve", lambda e, t_=t_, m=m: e.tensor_scalar(out=t_[:, m, :], in0=t_[:, m, :], scalar1=cmt[:, m:m + 1],
                                                                                  scalar2=8.0, op0=ALU.subtract, op1=ALU.mult),
                                     reads=[cmB], writes=[tB_])
                        P.op("dve", lambda e: e.tensor_scalar(out=subg[:], in0=subg[:], scalar1=1.0 - LAMBDA_INIT, scalar2=None, op0=ALU.mult),
                             writes=[subgB])
                        for i in range(1):
                            P.op("pool", lambda e, i=i: e.memset(qz[i][0][64:128, :, :], 0.0), writes=[qzB[i]])
                            P.op("pool", lambda e, i=i: e.memset(qz[i][1][0:64, :, :], 0.0), writes=[qzB[i]])
                        Sb = [psum[0], psum[1], psum[2], psum[7]]; SbB = [psB[0], psB[1], psB[2], psB[7]]
                        Ob = [[psum[3], psum[4]], [psum[5], psum[6]]]; ObB = [[psB[3], psB[4]], [psB[5], psB[6]]]
                        qTv = qT_s.rearrange("(c p) n -> p c n", p=128)
                        yrv = yr_s.rearrange("(c p) n -> p c n", p=128)
                        sc_ = {"s": 0, "p": 0, "st": 0, "ep": 0}
                        LOOK = 3
                        pending = []
                        for g in range(NG):
                            qi = 0
                            P.dma("sp", qz[qi][0][0:64, :, :], qTv[0:64, :, g * 512:(g + 1) * 512],
                                  reads=[qTB[4 * g + i] for i in range(4)], writes=[qzB[qi]], sem=qzS[qi])
                            P.dma("sp", qz[qi][1][64:128, :, :], qTv[64:128, :, g * 512:(g + 1) * 512],
                                  reads=[qTB[4 * g + i] for i in range(4)], writes=[qzB[qi]], sem=qzS[qi])
                            nkb = 8 * g + 8
                            steps = [(c, e_, kb) for c in range(4) for e_ in range(2) for kb in range(nkb)]
                            info = {}

                            def emit_qk_exp(idx, g=g, qi=qi):
                                c, e_, kb = steps[idx]
                                m = 2 * c + e_
                                i = kb - 8 * g
                                s_lo = 0 if i < 0 else i // 2
                                ncols = (4 - s_lo) * 128
                                sn = sc_["s"] % 4
                                sc_["s"] += 1
                                pn = sc_["p"] % NPT
                                sc_["p"] += 1
                                info[idx] = (pn, s_lo)
                                P.op("pe", lambda e: e.matmul(
                                    Sb[sn][:, 0:ncols], lhsT=KT[:, c, kb * 128:(kb + 1) * 128], rhs=qz[qi][e_][:, c, s_lo * 128:512],
                                    start=True, stop=True),
                                    reads=[KTB[kb // 4], qzB[qi]], writes=[SbB[sn]])
                                if i >= 0:
                                    kind = (bp0 if kb == 0 else bp) if i % 2 == 0 else bd
                                    kindB = (bp0B if kb == 0 else bpB) if i % 2 == 0 else bdB
                                    P.op("dve", lambda e: e.tensor_tensor(out=Sb[sn][:, 0:128], in0=Sb[sn][:, 0:128], in1=kind[:, m, :], op=ALU.add),
                                         reads=[kindB], writes=[SbB[sn]])
                                if kb == 0:
                                    P.op("act", lambda e: e.activation(
                                        out=pt[pn][:, s_lo * 128:512], in_=Sb[sn][:, 0:ncols], func=AF.Exp, scale=0.125,
                                        bias=col0[:, 0:1]),
                                        reads=[col0B], writes=[SbB[sn], ptB[pn]])
                                else:
                                    P.op("act", lambda e: e.activation(
                                        out=pt[pn][:, s_lo * 128:512], in_=Sb[sn][:, 0:ncols], func=AF.Exp, scale=0.125),
                                        writes=[SbB[sn], ptB[pn]])

                            def emit_pv(idx, g=g, nkb=nkb):
                                c, e_, kb = steps[idx]
                                pn, s_lo = info.pop(idx)

                                def pv(e):
                                    ins = None
                                    for s in range(s_lo, 4):
                                        last = (kb == 8 * g + 2 * s + 1)
                                        o0 = (s % 2) * 256
                                        ins = e.matmul(Ob[e_][s // 2][:, o0:o0 + 129], lhsT=pt[pn][:, s * 128:(s + 1) * 128], rhs=V[:, kb, c, :],
                                                       start=(kb == 0 and s % 2 == 0), stop=last, skip_group_check=True)
                                    return ins
                                P.op("pe", pv, reads=[ptB[pn], VB[kb], VoneB], writes=[ObB[e_][b_] for b_ in range(s_lo // 2, 2)])
                                if kb != nkb - 1:
                                    return
                                for b_ in range(2):
                                    P.op("dve", lambda e, b_=b_: e.reciprocal(out=rc[:, 4 * e_ + 2 * b_:4 * e_ + 2 * b_ + 2],
                                                                              in_=Ob[e_][b_][:, 128:512:256]),
                                         writes=[ObB[e_][b_], rcB])
                                    for s in (2 * b_, 2 * b_ + 1):
                                        o0 = (s % 2) * 256
                                        P.op("dve", lambda e, s=s, o0=o0, b_=b_: e.tensor_scalar(
                                            out=om[e_][:, s, :], in0=Ob[e_][b_][:, o0:o0 + 128],
                                            scalar1=rc[:, 4 * e_ + s:4 * e_ + s + 1], scalar2=None, op0=ALU.mult),
                                            reads=[rcB], writes=[ObB[e_][b_], omB[e_]])
                                if e_ != 1:
                                    return
                                assert not pending
                                P.op("dve", lambda e: e.scalar_tensor_tensor(out=oh[:], in0=om[1][:], scalar=neglam[:, 0:1], in1=om[0][:],
                                                                             op0=ALU.mult, op1=ALU.add),
                                     reads=[omB[0], omB[1], neglamB], writes=[ohB])
                                P.op("pool", lambda e: e.tensor_tensor(out=osq[:], in0=oh[:], in1=oh[:], op=ALU.mult), reads=[ohB], writes=[osqB])
                                P.op("dve", lambda e: e.tensor_reduce(out=oss[:, 0:4], in_=osq[:], axis=AX.X, op=ALU.add), reads=[osqB], writes=[ossB])
                                P.op("dve", lambda e: e.tensor_scalar(out=oss[:, 4:8], in0=oss[:, 0:4], scalar1=1.0 / 128, scalar2=EPS,
                                                                      op0=ALU.mult, op1=ALU.add), writes=[ossB])
                                P.op("pool", lambda e: e.tensor_tensor(out=oss[:, 8:12], in0=oss[:, 4:8], in1=neghalf4[:, 0:4], op=ALU.pow),
                                     reads=[neghalfB], writes=[ossB])
                                for s in range(4):
                                    P.op("dve", lambda e, s=s: e.scalar_tensor_tensor(
                                        out=oall[:, s, c * 128:(c + 1) * 128], in0=oh[:, s, :], scalar=oss[:, 8 + s:9 + s], in1=subg[:],
                                        op0=ALU.mult, op1=ALU.mult),
                                        reads=[ohB, ossB, subgB], writes=[oallB[s]])

                            nst = len(steps)
                            for idx in range(nst + LOOK):
                                if pending and idx % 2 == 1:
                                    pending.pop(0)()
                                if idx < nst:
                                    emit_qk_exp(idx)
                                if idx >= LOOK:
                                    emit_pv(idx - LOOK)
                            assert not pending
                            def make_pieces(g):
                                pieces = []
                                loads = []
                                for s in range(4):
                                    ob = 4 * g + s
                                    en = sc_["ep"] % 2
                                    sc_["ep"] += 1
                                    pblk = 2 * ob + 1

                                    def p_load(ob=ob, en=en, pblk=pblk):
                                        P.dma("sp", yT[en][:], yrv[:, :, ob * 128:(ob + 1) * 128], reads=[yrB[ob]], writes=[yTB[en]], sem=yTS[en])
                                        P.dma("sp", x1t[en][:], x1_s[pblk * 128:(pblk + 1) * 128, :], reads=[x1B[pblk]], writes=[x1tB[en]],
                                              sem=x1tS[en])
                                    loads.append(p_load)

                                    def p_tr(s=s, ob=ob, en=en, pblk=pblk):
                                        if s == 0:
                                            loads[0]()
                                        if s + 1 < 4:
                                            loads[s + 1]()

                                        bn = sc_["s"] % 4
                                        sc_["s"] += 1
                                        Tb, TbB = Sb[bn], SbB[bn]

                                        def tro(e):
                                            ins = None
                                            for c in range(4):
                                                ins = e.transpose(out=Tb[:, c * 128:(c + 1) * 128], in_=oall[:, s, c * 128:(c + 1) * 128],
                                                                  identity=ident[:])
                                            return ins
                                        P.op("pe", tro, reads=[oallB[s], identB], writes=[TbB])
                                        P.op("dve", lambda e: e.tensor_copy(out=oT[:], in_=Tb[:, :].rearrange("p (c t) -> p c t", c=4)),
                                             writes=[TbB, oTB])
                                    pieces.append(p_tr)
                                    for half in range(2):
                                        def p_mm(half=half, en=en, ob=ob):
                                            bn = sc_["s"] % 4
                                            sc_["s"] += 1
                                            Mb, MbB = Sb[bn], SbB[bn]

                                            def mmo(e):
                                                ins = None
                                                for c in range(4):
                                                    ins = e.matmul(Mb[:, :], lhsT=oT[:, c, :], rhs=wob[:, c, half * 512:(half + 1) * 512],
                                                                   start=(c == 0), stop=False)
                                                for c in range(4):
                                                    ins = e.matmul(Mb[:, :], lhsT=yT[en][:, c, :], rhs=wob[:, 4 + c, half * 512:(half + 1) * 512],
                                                                   start=False, stop=(c == 3))
                                                return ins
                                            P.op("pe", mmo, reads=[oTB, yTB[en], woB[half]], writes=[MbB])
                                            P.op("dve", lambda e: e.tensor_tensor(out=mtmp[half][:], in0=Mb[:, :],
                                                                                  in1=g2b[:, half * 512:(half + 1) * 512], op=ALU.mult),
                                                 reads=[g2B], writes=[MbB, mtmpB[half]])
                                            P.op("pool", lambda e: e.tensor_tensor(
                                                out=x1t[en][:, half * 512:(half + 1) * 512], in0=x1t[en][:, half * 512:(half + 1) * 512],
                                                in1=mtmp[half][:], op=ALU.add),
                                                reads=[mtmpB[half]], writes=[x1tB[en]])
                                            if half == 1:
                                                P.dma("sp", x2_s[ob * 128:(ob + 1) * 128, :], x1t[en][:], reads=[x1tB[en]], writes=[x2B[ob]],
                                                      sem=x1oS[en])
                                        pieces.append(p_mm)
                                return pieces
                            pending = make_pieces(g)
                        for f_ in pending:
                            f_()
                        P.end_phase()

        if "p4" in phases:
            ffn_phase("b", NOWN, lambda g: x2_s[g * 128:(g + 1) * 128, :], lambda g: x2B[g],
                      lambda g: out[g * 128:(g + 1) * 128, :], lambda g: outB,
                      f2w1, f2w3, f2w2, 2, 2, True, pre=(w1s, w3s, w2s, wsB) if "p2a" in phases else None)
        P.barrier()
        P.flush()
    return nc


def _t5_bucket_np(rel):
    n = np.maximum(rel, 0)
    max_exact = 16
    nf = np.maximum(n, 1).astype(np.float32)
    large = max_exact + (np.log(nf / max_exact) / math.log(128 / max_exact) * (32 - max_exact)).astype(np.int32)
    large = np.minimum(large, 31)
    return np.where(n < max_exact, n, large)


def make_in_maps(inputs, NBLK=64, batch_ids=None, n_cores=8):
    f32 = np.float32
    g = {k: np.asarray(v) for k, v in inputs.items()}
    S = NBLK * 128
    x = g["x"]
    shared = {}
    shared["ada_w"] = np.ascontiguousarray(g["ada_w"][0], dtype=f32)
    ada_b = g["ada_b"][0].astype(f32)
    shared["ada_b_pp"] = np.ascontiguousarray(ada_b.reshape(72, 128).T)
    shared["ada_b_gate"] = np.ascontiguousarray(np.broadcast_to(ada_b.reshape(9, D)[[2, 5, 8]][None], (128, 3, D)))
    shared["norm_g_pp"] = np.ascontiguousarray(g["norm_g"][0].astype(f32).reshape(3, KC, 128).transpose(2, 0, 1))
    shared["final_g_b"] = np.ascontiguousarray(np.broadcast_to(g["final_g"].astype(f32)[None], (128, D)))
    for a, b in (("f1w1", "ffn1_w1"), ("f1w3", "ffn1_w3"), ("f1w2", "ffn1_w2"),
                 ("f2w1", "ffn2_w1"), ("f2w3", "ffn2_w3"), ("f2w2", "ffn2_w2"), ("w_in", "w_in"), ("w_out", "w_out")):
        shared[a] = np.ascontiguousarray(g[b][0], dtype=f32)
    lam = np.stack([g["lam_q1"][0], g["lam_k1"][0], g["lam_q2"][0], g["lam_k2"][0]]).astype(f32)
    shared["lamv"] = np.ascontiguousarray(np.broadcast_to(lam[None], (128, 4, 64)))
    shared["subg_b"] = np.ascontiguousarray(np.broadcast_to(g["subln_g"][0].astype(f32)[None], (128, 128)))
    shared["conv_w_pp"] = np.ascontiguousarray(g["conv_w"][0].astype(f32).reshape(4, 4, 128).transpose(2, 1, 0))
    rp = np.stack([g["conv_b"][0], g["gate_a_b"][0], g["gate_i_b"][0], g["lru_L"][0]]).astype(f32)
    shared["rnn_pp"] = np.ascontiguousarray(rp.reshape(4, 4, 128).transpose(2, 1, 0))
    for nm, key in (("ga_bd", "gate_a_w"), ("gi_bd", "gate_i_w")):
        w = g[key][0].astype(f32)
        bdm = np.zeros((128, 4, 128), f32)
        for c in range(4):
            for hb in range(2):
                bdm[hb * 64:(hb + 1) * 64, c, hb * 64:(hb + 1) * 64] = w[2 * c + hb]
        shared[nm] = bdm
    rb = g["rel_bias"].astype(f32)
    kk = np.arange(128)[:, None]
    qq = np.arange(128)[None, :]
    rel_d = qq - kk
    rel_p = qq - kk + 128
    bd = rb[_t5_bucket_np(rel_d)]
    bd = np.where((rel_d >= 0)[:, :, None], bd, f32(NEG))
    bp = rb[_t5_bucket_np(rel_p)]
    shared["bdm"] = np.ascontiguousarray(bd.transpose(0, 2, 1).astype(f32))
    shared["bpm"] = np.ascontiguousarray(bp.transpose(0, 2, 1).astype(f32))
    shared["cm"] = np.ascontiguousarray(np.broadcast_to(rb[31][None], (128, 8)))
    if batch_ids is None:
        batch_ids = [c // 2 for c in range(n_cores)]
    maps = []
    for core in range(n_cores):
        b = batch_ids[core]
        h = core % 2
        m = dict(shared)
        xs = np.zeros((S, D), f32)
        if h == 0:
            xs[128:] = x[b, :S - 128]
            m["m0"] = np.zeros((128, 128), f32)
            m["bp0m"] = np.full((128, 8, 128), NEG, f32)
            m["col0"] = np.full((128, 1), NEG, f32)
        else:
            xs[:] = x[b, :S]
            m["m0"] = np.ones((128, 128), f32)
            m["bp0m"] = shared["bpm"]
            m["col0"] = np.zeros((128, 1), f32)
        m["xs"] = xs
        m["cvec"] = np.ascontiguousarray(g["c"][b].astype(f32).reshape(KC, 128).T)
        maps.append(m)
    return maps


_NC_CACHE = {}


def kernel(**inputs):
    NBLK = 64
    if NBLK not in _NC_CACHE:
        _NC_CACHE[NBLK] = build(NBLK)
    nc = _NC_CACHE[NBLK]
    maps = make_in_maps(inputs, NBLK)
    res = run_bass_kernel_spmd(nc, maps, core_ids=list(range(8)))
    B = 4
    S = 8192
    outp = np.zeros((B, S, D), np.float32)
    for core in range(8):
        b, h = core // 2, core % 2
        o = np.asarray(res.results[core]["out"]).reshape(32, 128, D)
        outp[b].reshape(64, 128, D)[h::2] = o
    return outp
```

```python
import math
from contextlib import ExitStack

import numpy as np
import concourse.bass as bass
import concourse.mybir as mybir
from concourse.bass_utils import run_bass_kernel_spmd

F32 = mybir.dt.float32
BF16 = mybir.dt.bfloat16
AF = mybir.ActivationFunctionType
ALU = mybir.AluOpType
AX = mybir.AxisListType

D = 1024
DFF = 2816
NF = 22
KC = 8
DIN = 2560
EPS = 1e-6
NEG = -30000.0
LAMBDA_INIT = 0.8 - 0.6 * math.exp(0.0)


class Sem:
    def __init__(self, h, name):
        self.h = h
        self.n = 0
        self.name = name


class Buf:
    __slots__ = ("w", "r", "name")

    def __init__(self, name=""):
        self.w = {}
        self.r = {}
        self.name = name


class Prog:
    ENG = ("pe", "act", "dve", "pool", "sp")

    def __init__(self, nc, st, n_dma_sems=66):
        self.nc = nc
        self.st = st
        self.ops = {k: [] for k in self.ENG}
        self.esem = {}
        self.allsems = []
        self.seen = {k: {} for k in self.ENG}
        self.dsems = []
        for i in range(n_dma_sems):
            s = Sem(st.enter_context(nc.semaphore("ds%d" % i)), "d%d" % i)
            self.dsems.append(s)
            self.allsems.append(s)
        self.dnext = 0
        self.phase = 0
        self.new_engine_sems()

    def new_engine_sems(self):
        for k in self.ENG:
            s = Sem(self.st.enter_context(self.nc.semaphore("es%d_%s" % (self.phase, k))), k)
            self.esem[k] = s
            self.allsems.append(s)
        self.phase += 1

    def dma_sem(self):
        s = self.dsems[self.dnext]
        self.dnext += 1
        return s

    def op(self, eng, fn, reads=(), writes=(), sem=None, inc=1):
        waits = {}
        for b in reads:
            for s, v in b.w.items():
                if waits.get(s, 0) < v:
                    waits[s] = v
        for b in writes:
            for s, v in b.w.items():
                if waits.get(s, 0) < v:
                    waits[s] = v
            for s, v in b.r.items():
                if waits.get(s, 0) < v:
                    waits[s] = v
        seen = self.seen[eng]
        own = self.esem[eng]
        wl = []
        for s, v in waits.items():
            if eng == "pe" and s is own:
                continue
            if seen.get(s, 0) >= v:
                continue
            seen[s] = v
            wl.append((s.h, v))
        S = sem if sem is not None else own
        S.n += inc
        self.ops[eng].append((wl, fn, S.h, inc))
        for b in reads:
            if b.r.get(S, 0) < S.n:
                b.r[S] = S.n
        for b in writes:
            b.w = {S: S.n}
            b.r = {}
        return (S, S.n)

    def dma(self, eng, out, in_, reads=(), writes=(), sem=None, **kw):
        return self.op(eng, lambda e: e.dma_start(out=out, in_=in_, **kw),
                       reads=reads, writes=writes, sem=sem, inc=16)

    def dma_group(self, sem, items):
        bufs = []
        for it_ in items:
            eng, out, in_, writes = it_[:4]
            reads = it_[4] if len(it_) > 4 else ()
            self.dma(eng, out, in_, reads=reads, writes=writes, sem=sem)
            bufs.extend(writes)
        for b in bufs:
            b.w = {sem: sem.n}

    def barrier(self):
        evs = [(s, s.n) for s in self.allsems if s.n > 0]
        for k in self.ENG:
            seen = self.seen[k]
            wl = []
            for s, v in evs:
                if seen.get(s, 0) >= v:
                    continue
                seen[s] = v
                wl.append((s.h, v))
            if wl:
                self.ops[k].append((wl, None, None, 0))

    def flush(self):
        nc = self.nc
        ops = self.ops
        self.ops = {k: [] for k in self.ENG}

        def run(e, lst):
            for wl, fn, sh, inc in lst:
                for h, v in wl:
                    e.wait_ge(h, v)
                if fn is not None:
                    fn(e).then_inc(sh, inc)

        with nc.Block() as block:
            @block.tensor
            def _(e):
                run(e, ops["pe"])

            @block.scalar
            def _(e):
                run(e, ops["act"])

            @block.vector
            def _(e):
                run(e, ops["dve"])

            @block.gpsimd
            def _(e):
                run(e, ops["pool"])

            @block.sync
            def _(e):
                run(e, ops["sp"])

    def end_phase(self):
        self.barrier()
        self.flush()
        self.new_engine_sems()


def build(NBLK=64, phases=("p0", "p1", "p2a", "p2b", "p3", "p4"), debug=False):
    S = NBLK * 128
    NOWN = NBLK // 2
    SOWN = NOWN * 128
    NG = NBLK // 8
    nc = bass.Bass("TRN2", target_bir_lowering=False)

    def din(name, shape, dt=F32):
        return nc.dram_tensor(name, list(shape), dt, kind="ExternalInput").ap()

    xs = din("xs", [S, D])
    m0 = din("m0", [128, 128])
    cvec = din("cvec", [128, KC])
    ada_w = din("ada_w", [D, 9 * D])
    ada_b_pp = din("ada_b_pp", [128, 72])
    ada_b_gate = din("ada_b_gate", [128, 3, D])
    norm_g_pp = din("norm_g_pp", [128, 3, KC])
    final_g_b = din("final_g_b", [128, D])
    f1w1 = din("f1w1", [D, DFF]); f1w3 = din("f1w3", [D, DFF]); f1w2 = din("f1w2", [DFF, D])
    f2w1 = din("f2w1", [D, DFF]); f2w3 = din("f2w3", [D, DFF]); f2w2 = din("f2w2", [DFF, D])
    w_in = din("w_in", [D, DIN])
    w_out = din("w_out", [D, D])
    lamv = din("lamv", [128, 4, 64])
    subg_b = din("subg_b", [128, 128])
    conv_w_pp = din("conv_w_pp", [128, 4, 4])
    rnn_pp = din("rnn_pp", [128, 4, 4])
    ga_bd = din("ga_bd", [128, 4, 128])
    gi_bd = din("gi_bd", [128, 4, 128])
    bdm = din("bdm", [128, 8, 128])
    bpm = din("bpm", [128, 8, 128])
    bp0m = din("bp0m", [128, 8, 128])
    cm_in = din("cm", [128, 8])
    col0_in = din("col0", [128, 1])

    out = nc.dram_tensor("out", [SOWN, D], F32, kind="ExternalOutput").ap()
    x1_s = nc.dram_tensor("x1_s", [S, D], F32, kind="Internal").ap()
    x2_s = nc.dram_tensor("x2_s", [SOWN, D], F32, kind="Internal").ap()
    yr_s = nc.dram_tensor("yr_s", [512, SOWN], BF16, kind="Internal").ap()
    qT_s = nc.dram_tensor("qT_s", [512, SOWN], BF16, kind="Internal").ap()
    gate_s = nc.dram_tensor("gate_s", [3, 128, D], F32, kind="Internal").ap()
    w1s = nc.dram_tensor("w1s", [D, DFF], BF16, kind="Internal").ap()
    w3s = nc.dram_tensor("w3s", [D, DFF], BF16, kind="Internal").ap()
    w2s = nc.dram_tensor("w2s", [DFF, D], BF16, kind="Internal").ap()
    dbg = {}
    if debug:
        dbg["x1"] = nc.dram_tensor("dbg_x1", [S, D], F32, kind="ExternalOutput").ap()
        dbg["x2"] = nc.dram_tensor("dbg_x2", [SOWN, D], F32, kind="ExternalOutput").ap()
        dbg["yr"] = nc.dram_tensor("dbg_yr", [512, SOWN], BF16, kind="ExternalOutput").ap()
        dbg["qT"] = nc.dram_tensor("dbg_qT", [512, SOWN], BF16, kind="ExternalOutput").ap()
        dbg["kT"] = nc.dram_tensor("dbg_kT", [128, 4, S], BF16, kind="ExternalOutput").ap()
        dbg["v"] = nc.dram_tensor("dbg_v", [128, NBLK, 4, 129], BF16, kind="ExternalOutput").ap()
        dbg["mod"] = nc.dram_tensor("dbg_mod", [128, 6, KC], F32, kind="ExternalOutput").ap()
        dbg["gate"] = nc.dram_tensor("dbg_gate", [3, 128, D], F32, kind="ExternalOutput").ap()
        dbg["oall"] = nc.dram_tensor("dbg_oall", [SOWN, 512], F32, kind="ExternalOutput").ap()

    with ExitStack() as st:
        P = Prog(nc, st)

        def sb(stk, name, shape, dt=F32):
            return stk.enter_context(nc.sbuf_tensor("sb_" + name, list(shape), dt))

        ident = sb(st, "ident", [128, 128]); identB = Buf("ident")
        gm = sb(st, "gm", [128, 3, KC]); gmB = Buf("gm")
        shv = sb(st, "shv", [128, 3, KC]); shB = Buf("shv")
        neglam = sb(st, "neglam", [128, 1]); neglamB = Buf("neglam")
        neghalf4 = sb(st, "neghalf", [128, 4]); neghalfB = Buf("neghalf")
        neghalf = neghalf4
        P.op("dve", lambda e: e.memset(neghalf4[:], -0.5), writes=[neghalfB])
        psum = [st.enter_context(nc.psum_tensor("ps%d" % i, [128, 512], F32)) for i in range(8)]
        psB = [Buf("ps%d" % i) for i in range(8)]
        x1B = [Buf("x1_%d" % i) for i in range(NBLK)]
        x2B = [Buf("x2_%d" % i) for i in range(NOWN)]
        yrB = [Buf("yr_%d" % i) for i in range(NOWN)]
        qTB = [Buf("qT_%d" % i) for i in range(NOWN)]
        gateB = [Buf("gate%d" % i) for i in range(3)]
        outB = Buf("out")
        wsB = {"w1": [Buf(), Buf()], "w3": [Buf(), Buf()], "w2": [Buf(), Buf()]}
        wsS = P.dma_sem()

        def precast_ffn2_weights():
            items = []
            for nm, src, dst in (("w1", f2w1, w1s), ("w3", f2w3, w3s)):
                for hf in range(2):
                    items.append(("pool", dst[:, hf * 1408:(hf + 1) * 1408], src[:, hf * 1408:(hf + 1) * 1408], [wsB[nm][hf]]))
            for hf in range(2):
                items.append(("pool", w2s[hf * 1408:(hf + 1) * 1408, :], f2w2[hf * 1408:(hf + 1) * 1408, :], [wsB["w2"][hf]]))
            P.dma_group(wsS, items)

        P.op("pool", lambda e: e.memset(ident[:], 0.0), writes=[identB])
        P.op("pool", lambda e: e.affine_select(out=ident[:], in_=ident[:], compare_op=ALU.not_equal,
                                               fill=1.0, base=0, pattern=[[-1, 128]], channel_multiplier=1),
             reads=[identB], writes=[identB])

        if "p0" in phases:
            with ExitStack() as ph:
                cv = sb(ph, "cv", [128, KC]); cvB = Buf()
                cact2 = sb(ph, "cact2", [128, KC, 2]); cact2B = Buf()
                ones = sb(ph, "ones", [128, 128]); onesB = Buf()
                crep = sb(ph, "crep", [128, KC, 128]); crepB = Buf()
                abpp = sb(ph, "abpp", [128, 72]); abppB = Buf()
                ngpp = sb(ph, "ngpp", [128, 3, KC]); ngppB = Buf()
                modpp = sb(ph, "modpp", [128, 6, KC]); modppB = Buf()
                awt = [sb(ph, "awt%d" % i, [128, KC, D]) for i in range(2)]
                awtB = [Buf(), Buf()]
                awtS = [P.dma_sem(), P.dma_sem()]
                abg = sb(ph, "abg", [128, D]); abgB = Buf(); abgS = P.dma_sem()
                gt = sb(ph, "gt", [128, D]); gtB = Buf()
                lv = sb(ph, "lv", [128, 4, 64]); lvB = Buf()
                lp = sb(ph, "lp", [128, 2, 64]); lpB = Buf()
                ls = sb(ph, "ls", [128, 2]); lsB = Buf()
                le = sb(ph, "le", [128, 2]); leB = Buf()
                s0 = P.dma_sem()
                P.dma_group(s0, [("sp", cv[:], cvec, [cvB]), ("sp", abpp[:], ada_b_pp, [abppB]),
                                 ("sp", ngpp[:], norm_g_pp, [ngppB]), ("sp", lv[:], lamv, [lvB])])
                P.op("act", lambda e: e.activation(out=cact2[:, :, 0], in_=cv[:], func=AF.Silu), reads=[cvB], writes=[cact2B])
                P.op("act", lambda e: e.activation(out=cact2[:, :, 1], in_=cv[:], func=AF.Silu), reads=[cvB], writes=[cact2B])
                P.op("dve", lambda e: e.memset(ones[:], 1.0), writes=[onesB])
                for k in range(KC):
                    P.op("dve", lambda e, k=k: e.tensor_scalar(out=crep[:, k, :], in0=ones[:], scalar1=cact2[:, k, 0:1],
                                                               scalar2=None, op0=ALU.mult),
                         reads=[onesB, cact2B], writes=[crepB])
                P.op("dve", lambda e: e.tensor_tensor(out=lp[:, 0, :], in0=lv[:, 0, :], in1=lv[:, 1, :], op=ALU.mult), reads=[lvB], writes=[lpB])
                P.op("dve", lambda e: e.tensor_tensor(out=lp[:, 1, :], in0=lv[:, 2, :], in1=lv[:, 3, :], op=ALU.mult), reads=[lvB], writes=[lpB])
                P.op("dve", lambda e: e.tensor_reduce(out=ls[:], in_=lp[:], axis=AX.X, op=ALU.add), reads=[lpB], writes=[lsB])
                P.op("act", lambda e: e.activation(out=le[:], in_=ls[:], func=AF.Exp), reads=[lsB], writes=[leB])
                P.op("dve", lambda e: e.tensor_tensor(out=neglam[:], in0=le[:, 1:2], in1=le[:, 0:1], op=ALU.subtract), reads=[leB], writes=[neglamB])
                P.op("dve", lambda e: e.tensor_scalar(out=neglam[:], in0=neglam[:], scalar1=-LAMBDA_INIT, scalar2=None, op0=ALU.add),
                     reads=[neglamB], writes=[neglamB])
                ppi = 0
                for j in range(9):
                    slot = j % 2
                    P.dma("sp", awt[slot][:], ada_w[:, j * D:(j + 1) * D].rearrange("(k p) f -> p k f", p=128),
                          writes=[awtB[slot]], sem=awtS[slot])
                    if j % 3 != 2:
                        bank = psum[0]

                        def mm(e, slot=slot, bank=bank):
                            ins = None
                            for dc in range(KC):
                                for k in range(KC):
                                    ins = e.matmul(bank[:, 2 * dc:2 * dc + 2], lhsT=awt[slot][:, k, dc * 128:(dc + 1) * 128],
                                                   rhs=cact2[:, k, :], start=(k == 0), stop=(k == KC - 1))
                            return ins
                        P.op("pe", mm, reads=[awtB[slot], cact2B], writes=[psB[0]])
                        P.op("dve", lambda e, ppi=ppi, j=j, bank=bank: e.tensor_tensor(
                            out=modpp[:, ppi, :], in0=bank[:, 0:16].rearrange("p (d two) -> p d two", two=2)[:, :, 0],
                            in1=abpp[:, j * 8:(j + 1) * 8], op=ALU.add),
                            reads=[abppB], writes=[psB[0], modppB])
                        ppi += 1
                    else:
                        gi = j // 3
                        P.dma("sp", abg[:], ada_b_gate[:, gi, :], writes=[abgB], sem=abgS)
                        for half in range(2):
                            bank = psum[1 + half]

                            def mmg(e, slot=slot, bank=bank, half=half):
                                ins = None
                                for k in range(KC):
                                    ins = e.matmul(bank[:, :], lhsT=crep[:, k, :], rhs=awt[slot][:, k, half * 512:(half + 1) * 512],
                                                   start=(k == 0), stop=(k == KC - 1))
                                return ins
                            P.op("pe", mmg, reads=[awtB[slot], crepB], writes=[psB[1 + half]])
                            P.op("dve", lambda e, bank=bank, half=half: e.tensor_tensor(
                                out=gt[:, half * 512:(half + 1) * 512], in0=bank[:, :], in1=abg[:, half * 512:(half + 1) * 512], op=ALU.add),
                                reads=[abgB], writes=[psB[1 + half], gtB])
                        if gi != 1:
                            P.op("dve", lambda e: e.tensor_scalar(out=gt[:], in0=gt[:], scalar1=0.5, scalar2=None, op0=ALU.mult),
                                 reads=[gtB], writes=[gtB])
                        P.dma("sp", gate_s[gi], gt[:], reads=[gtB], writes=[gateB[gi]], sem=P.dma_sem())
                        if debug:
                            P.dma("sp", dbg["gate"][gi], gt[:], reads=[gtB], writes=[dbgB], sem=P.dma_sem())
                for i in range(3):
                    P.op("dve", lambda e, i=i: e.scalar_tensor_tensor(out=gm[:, i, :], in0=modpp[:, 2 * i + 1, :], scalar=1.0,
                                                                      in1=ngpp[:, i, :], op0=ALU.add, op1=ALU.mult),
                         reads=[modppB, ngppB], writes=[gmB])
                    P.op("dve", lambda e, i=i: e.tensor_copy(out=shv[:, i, :], in_=modpp[:, 2 * i, :]), reads=[modppB], writes=[shB])
                if debug:
                    P.dma("sp", dbg["mod"], modpp[:], reads=[modppB], writes=[dbgB], sem=P.dma_sem())
                P.end_phase()

        def load_w_bf16(dst, dstB, src_pkf, nk, nf, fchunk, sem):
            for f0 in range(0, nf, fchunk):
                f1 = min(nf, f0 + fchunk)
                P.dma("pool", dst[:, :, f0:f1], src_pkf[:, :, f0:f1], writes=[dstB[f0 // fchunk]], sem=sem)

        def norm_part(xslot, xB, xn, xnB, ssq, ssqB, lnexp=False, noact=False):
            if noact:
                P.op("dve", lambda e: e.scalar_tensor_tensor(out=xn[:], in0=xslot, scalar=1.0, in1=xslot, op0=ALU.mult, op1=ALU.mult,
                                                             accum_out=ssq[:, 0:1]),
                     reads=[xB], writes=[xnB, ssqB])
                P.op("dve", lambda e: e.tensor_scalar(out=ssq[:, 1:2], in0=ssq[:, 0:1], scalar1=1.0 / D, scalar2=EPS, op0=ALU.mult, op1=ALU.add),
                     reads=[ssqB], writes=[ssqB])
                P.op("pool", lambda e: e.tensor_tensor(out=ssq[:, 2:3], in0=ssq[:, 1:2], in1=neghalf[:, 0:1], op=ALU.pow),
                     reads=[ssqB, neghalfB], writes=[ssqB])
            else:
                P.op("act", lambda e: e.activation(out=xn[:], in_=xslot, func=AF.Square, accum_out=ssq[:, 0:1]),
                     reads=[xB], writes=[xnB, ssqB])
                if lnexp:
                    P.op("act", lambda e: e.activation(out=ssq[:, 1:2], in_=ssq[:, 0:1], func=AF.Ln, scale=1.0 / D, bias=EPS),
                         reads=[ssqB], writes=[ssqB])
                    P.op("act", lambda e: e.activation(out=ssq[:, 2:3], in_=ssq[:, 1:2], func=AF.Exp, scale=-0.5),
                         reads=[ssqB], writes=[ssqB])
                else:
                    P.op("act", lambda e: e.activation(out=ssq[:, 1:2], in_=ssq[:, 0:1], func=AF.Sqrt, scale=1.0 / D, bias=EPS),
                         reads=[ssqB], writes=[ssqB])
                    P.op("dve", lambda e: e.reciprocal(out=ssq[:, 2:3], in_=ssq[:, 1:2]), reads=[ssqB], writes=[ssqB])
            P.op("pool", lambda e: e.tensor_scalar(out=xn[:], in0=xslot, scalar1=ssq[:, 2:3], scalar2=0.0, op0=ALU.mult, op1=ALU.add),
                 reads=[xB, ssqB], writes=[xnB])

        def tr_round(xn, xnB, i_norm, hT, hTB, col0, tbank, tB, r):
            def tr(e):
                ins = None
                for kk in range(4):
                    k = 4 * r + kk
                    ins = e.transpose(out=tbank[:, kk * 128:(kk + 1) * 128], in_=xn[:, k * 128:(k + 1) * 128], identity=ident[:])
                return ins
            P.op("pe", tr, reads=[xnB, identB], writes=[tB])
            for kk in range(4):
                k = 4 * r + kk
                if r % 2 == 0:
                    P.op("dve", lambda e, k=k, kk=kk: e.tensor_scalar(
                        out=hT[:, k, col0:col0 + 128], in0=tbank[:, kk * 128:(kk + 1) * 128],
                        scalar1=gm[:, i_norm, k:k + 1], scalar2=shv[:, i_norm, k:k + 1], op0=ALU.mult, op1=ALU.add),
                        reads=[gmB, shB], writes=[tB, hTB[k]])
                else:
                    P.op("act", lambda e, k=k, kk=kk: e.activation(
                        out=hT[:, k, col0:col0 + 128], in_=tbank[:, kk * 128:(kk + 1) * 128], func=AF.Identity,
                        scale=gm[:, i_norm, k:k + 1], bias=shv[:, i_norm, k:k + 1]),
                        reads=[gmB, shB], writes=[tB, hTB[k]])

        def tr_part(xn, xnB, i_norm, hT, hTB, col0, tbanks, tBs):
            for r in range(2):
                tbank, tB = tbanks[r], tBs[r]

                def tr(e, r=r, tbank=tbank):
                    ins = None
                    for kk in range(4):
                        k = 4 * r + kk
                        ins = e.transpose(out=tbank[:, kk * 128:(kk + 1) * 128], in_=xn[:, k * 128:(k + 1) * 128], identity=ident[:])
                    return ins
                P.op("pe", tr, reads=[xnB, identB], writes=[tB])
                for kk in range(4):
                    k = 4 * r + kk
                    if r % 2 == 0:
                        P.op("dve", lambda e, k=k, kk=kk, tbank=tbank: e.tensor_scalar(
                            out=hT[:, k, col0:col0 + 128], in0=tbank[:, kk * 128:(kk + 1) * 128],
                            scalar1=gm[:, i_norm, k:k + 1], scalar2=shv[:, i_norm, k:k + 1], op0=ALU.mult, op1=ALU.add),
                            reads=[gmB, shB], writes=[tB, hTB[k]])
                    else:
                        P.op("act", lambda e, k=k, kk=kk, tbank=tbank: e.activation(
                            out=hT[:, k, col0:col0 + 128], in_=tbank[:, kk * 128:(kk + 1) * 128], func=AF.Identity,
                            scale=gm[:, i_norm, k:k + 1], bias=shv[:, i_norm, k:k + 1]),
                            reads=[gmB, shB], writes=[tB, hTB[k]])

        def ffn_phase(tag, n_groups, src_rows, srcB, dst_rows, dstB, w1, w3, w2, i_norm, gate_idx, final, dbg_rows=None, pre=None):
            TG = 2
            TT = TG * 128
            n_tiles = n_groups // TG
            with ExitStack() as ph:
                w1b = sb(ph, tag + "w1b", [128, KC, DFF], BF16)
                w3b = sb(ph, tag + "w3b", [128, KC, DFF], BF16)
                w2b = sb(ph, tag + "w2b", [128, NF, D], BF16)
                FCH = 704
                w1B = [Buf() for _ in range(4)]; w3B = [Buf() for _ in range(4)]
                w2B = [Buf() for _ in range(2)]
                gb = sb(ph, tag + "gb", [128, D]); gbB = Buf()
                NSLOT = 6
                xr = [sb(ph, tag + "xr%d" % i, [128, D]) for i in range(NSLOT)]
                xrB = [Buf() for _ in range(NSLOT)]
                xrS = [P.dma_sem() for _ in range(NSLOT)]
                stS = xrS
                xn = [sb(ph, tag + "xn%d" % i, [128, D]) for i in range(2)]
                xnB = [Buf(), Buf()]
                ssq = [sb(ph, tag + "ssq%d" % i, [128, 4]) for i in range(2)]
                ssqB = [Buf(), Buf()]
                hT = sb(ph, tag + "hT", [128, KC, TT], BF16); hTB = [Buf() for _ in range(KC)]
                uT = sb(ph, tag + "uT", [128, NF, TT], BF16); uTB = [Buf() for _ in range(NF)]
                sl = [sb(ph, tag + "sl%d" % i, [128, TT]) for i in range(2)]
                slB = [Buf(), Buf()]
                tmp = [sb(ph, tag + "tmp%d" % i, [128, 512]) for i in range(2)]
                tmpB = [Buf(), Buf()]
                if final:
                    fg = sb(ph, tag + "fg", [128, D]); fgB = Buf()
                    fjunk = sb(ph, tag + "fjunk", [128, D]); fjunkB = Buf()
                    fsq = [sb(ph, tag + "fsq%d" % i, [128, 4]) for i in range(2)]; fsqB = [Buf(), Buf()]
                    P.dma("sp", fg[:], final_g_b, writes=[fgB], sem=P.dma_sem())
                w1v = w1.rearrange("(k p) f -> p k f", p=128)
                w3v = w3.rearrange("(k p) f -> p k f", p=128)
                w2v = w2.rearrange("(f p) d -> p f d", p=128)
                def load_weights():
                  if pre is None:
                    for q in range(4):
                        P.dma_group(P.dma_sem(), [("pool", w1b[:, :, q * FCH:(q + 1) * FCH], w1v[:, :, q * FCH:(q + 1) * FCH], [w1B[q]]),
                                                  ("pool", w3b[:, :, q * FCH:(q + 1) * FCH], w3v[:, :, q * FCH:(q + 1) * FCH], [w3B[q]])])
                    P.dma_group(P.dma_sem(), [("pool", w2b[:, q * 11:(q + 1) * 11, :], w2v[:, q * 11:(q + 1) * 11, :], [w2B[q]]) for q in range(2)])
                  else:
                    p1, p3, p2, pB = pre
                    p1v = p1.rearrange("(k p) f -> p k f", p=128)
                    p3v = p3.rearrange("(k p) f -> p k f", p=128)
                    p2v = p2.rearrange("(f p) d -> p f d", p=128)
                    for q in range(4):
                        P.dma_group(P.dma_sem(), [("sp", w1b[:, :, q * FCH:(q + 1) * FCH], p1v[:, :, q * FCH:(q + 1) * FCH], [w1B[q]], pB["w1"]),
                                                  ("sp", w3b[:, :, q * FCH:(q + 1) * FCH], p3v[:, :, q * FCH:(q + 1) * FCH], [w3B[q]], pB["w3"])])
                    P.dma_group(P.dma_sem(), [("sp", w2b[:, q * 11:(q + 1) * 11, :], p2v[:, q * 11:(q + 1) * 11, :], [w2B[q]], pB["w2"])
                                              for q in range(2)])
                P.dma("sp", gb[:], gate_s[gate_idx], reads=[gateB[gate_idx]], writes=[gbB], sem=P.dma_sem())
                A = [psum[0], psum[1]]; AB = [psB[0], psB[1]]
                Bk = [psum[2], psum[3]]; BB = [psB[2], psB[3]]
                Dk = [psum[4], psum[5]]; DB = [psB[4], psB[5]]
                Tk = [psum[6], psum[7]]; TB = [psB[6], psB[7]]
                cnt = {"g": 0, "f": 0, "d": 0}

                def stage_N(t):
                    for j in range(TG):
                        gidx = t * TG + j
                        slot = gidx % NSLOT
                        P.dma("sp", xr[slot][:], src_rows(gidx), reads=[srcB(gidx)], writes=[xrB[slot]], sem=xrS[slot])
                        norm_part(xr[slot][:], xrB[slot], xn[j], xnB[j], ssq[j], ssqB[j], noact=True)

                def x_rounds(t):
                    return [(lambda j=j, r=r: tr_round(xn[j], xnB[j], i_norm, hT, hTB, j * 128, Tk[r], TB[r], r))
                            for j in range(TG) for r in range(2)]

                def stage_X(t):
                    for f_ in x_rounds(t):
                        f_()

                def stage_U(t):
                    for f in range(NF):
                        if f == 3 and t + 1 < n_tiles:
                            stage_N(t + 1)
                        n = cnt["f"] % 2
                        cnt["f"] += 1
                        q = (f * 128) // FCH
                        q2 = (f * 128 + 127) // FCH
                        wr1 = [w1B[q]] + ([w1B[q2]] if q2 != q else [])
                        wr3 = [w3B[q]] + ([w3B[q2]] if q2 != q else [])

                        def mma(e, f=f, n=n):
                            ins = None
                            for k in range(KC):
                                ins = e.matmul(A[n][:, 0:TT], lhsT=w1b[:, k, f * 128:(f + 1) * 128], rhs=hT[:, k, :],
                                               start=(k == 0), stop=(k == KC - 1))
                            return ins

                        def mmb(e, f=f, n=n):
                            ins = None
                            for k in range(KC):
                                ins = e.matmul(Bk[n][:, 0:TT], lhsT=w3b[:, k, f * 128:(f + 1) * 128], rhs=hT[:, k, :],
                                               start=(k == 0), stop=(k == KC - 1))
                            return ins
                        P.op("pe", mma, reads=wr1 + hTB, writes=[AB[n]])
                        P.op("pe", mmb, reads=wr3 + hTB, writes=[BB[n]])
                        P.op("act", lambda e, n=n: e.activation(out=sl[n][:], in_=A[n][:, 0:TT], func=AF.Silu),
                             writes=[AB[n], slB[n]])
                        P.op("dve", lambda e, n=n, f=f: e.tensor_tensor(out=uT[:, f, :], in0=sl[n][:], in1=Bk[n][:, 0:TT], op=ALU.mult),
                             reads=[slB[n]], writes=[BB[n], uTB[f]])

                def stage_D(t, inter=()):
                    inter = list(inter)
                    for j in range(TG):
                        gidx = t * TG + j
                        slot = gidx % NSLOT
                        for half in range(2):
                            if inter:
                                inter.pop(0)()
                            n = cnt["d"] % 2
                            cnt["d"] += 1

                            def mmd(e, j=j, half=half, n=n):
                                ins = None
                                for f in range(NF):
                                    ins = e.matmul(Dk[n][:, :], lhsT=uT[:, f, j * 128:(j + 1) * 128],
                                                   rhs=w2b[:, f, half * 512:(half + 1) * 512], start=(f == 0), stop=(f == NF - 1))
                                return ins
                            P.op("pe", mmd, reads=uTB + w2B, writes=[DB[n]])
                            P.op("dve", lambda e, n=n, half=half: e.tensor_tensor(
                                out=tmp[n][:], in0=Dk[n][:, :], in1=gb[:, half * 512:(half + 1) * 512], op=ALU.mult),
                                reads=[gbB], writes=[DB[n], tmpB[n]])
                            P.op("pool", lambda e, n=n, half=half, slot=slot: e.tensor_tensor(
                                out=xr[slot][:, half * 512:(half + 1) * 512], in0=xr[slot][:, half * 512:(half + 1) * 512],
                                in1=tmp[n][:], op=ALU.add),
                                reads=[tmpB[n]], writes=[xrB[slot]])
                        if final:
                            n2 = cnt["g"] % 2
                            cnt["g"] += 1
                            xnf, xnfB, sq, sqB = fjunk, fjunkB, fsq[n2], fsqB[n2]
                            P.op("dve", lambda e, slot=slot, xnf=xnf, sq=sq: e.scalar_tensor_tensor(
                                out=xnf[:], in0=xr[slot][:], scalar=1.0, in1=xr[slot][:], op0=ALU.mult, op1=ALU.mult, accum_out=sq[:, 0:1]),
                                reads=[xrB[slot]], writes=[xnfB, sqB])
                            P.op("dve", lambda e, sq=sq: e.tensor_scalar(out=sq[:, 1:2], in0=sq[:, 0:1], scalar1=1.0 / D, scalar2=EPS,
                                                                         op0=ALU.mult, op1=ALU.add), reads=[sqB], writes=[sqB])
                            P.op("pool", lambda e, sq=sq: e.tensor_tensor(out=sq[:, 2:3], in0=sq[:, 1:2], in1=neghalf[:, 0:1], op=ALU.pow),
                                 reads=[sqB, neghalfB], writes=[sqB])
                            P.op("dve", lambda e, slot=slot, sq=sq: e.scalar_tensor_tensor(
                                out=xr[slot][:], in0=xr[slot][:], scalar=sq[:, 2:3], in1=fg[:], op0=ALU.mult, op1=ALU.mult),
                                reads=[sqB, fgB], writes=[xrB[slot]])
                        P.dma("sp", dst_rows(gidx), xr[slot][:], reads=[xrB[slot]], writes=[dstB(gidx)], sem=stS[slot])
                        if dbg_rows is not None:
                            P.dma("sp", dbg_rows(gidx), xr[slot][:], reads=[xrB[slot]], writes=[dbgB], sem=stS[slot])

                if pre is None:
                    load_weights()
                    stage_N(0)
                else:
                    stage_N(0)
                    load_weights()
                stage_X(0)
                for t in range(n_tiles):
                    stage_U(t)
                    stage_D(t, x_rounds(t + 1) if t + 1 < n_tiles else ())
                P.end_phase()

        if "p1" in phases:
            xsB = Buf("xs")
            ffn_phase("a", NBLK, lambda g: xs[g * 128:(g + 1) * 128, :], lambda g: xsB,
                      lambda g: x1_s[g * 128:(g + 1) * 128, :], lambda g: x1B[g],
                      f1w1, f1w3, f1w2, 0, 0, False,
                      dbg_rows=(lambda g: dbg["x1"][g * 128:(g + 1) * 128, :]) if debug else None)

        NT = NBLK // 4
        if "p2a" in phases:
            with ExitStack() as ph:
                wrb = sb(ph, "wrb", [128, KC, 1024], BF16); wrB = [Buf(), Buf()]
                NSLOT = 6
                xr = [sb(ph, "rxr%d" % i, [128, D]) for i in range(NSLOT)]
                xrB = [Buf() for _ in range(NSLOT)]
                xrS = [P.dma_sem() for _ in range(NSLOT)]
                xn = [sb(ph, "rxn%d" % i, [128, D]) for i in range(4)]; xnB = [Buf() for _ in range(4)]
                ssq = [sb(ph, "rssq%d" % i, [128, 4]) for i in range(4)]; ssqB = [Buf() for _ in range(4)]
                hT2 = [sb(ph, "rhT%d" % i, [128, KC, 512], BF16) for i in range(2)]; hT2B = [[Buf() for _ in range(KC)] for _ in range(2)]
                def dbl(name, shape):
                    return [sb(ph, name + "%d" % i, shape) for i in range(2)], [[Buf() for _ in range(4)] for _ in range(2)]
                xrt, xrtB = dbl("xrt", [128, 4, 515])
                cvt, cvtB = dbl("cvt", [128, 4, 512])
                rt, rtB = dbl("rt", [128, 4, 512])
                it, itB = dbl("it", [128, 4, 512])
                grs, grsB = dbl("grs", [128, 4, 256])
                gz, gzB = dbl("gz", [128, 4, 256])
                at = sb(ph, "at", [128, 4, 512]); atB = [Buf() for _ in range(4)]
                qt_ = sb(ph, "sqt", [128, 4, 512]); qtB_ = [Buf() for _ in range(4)]
                hh = sb(ph, "hh", [128, 4, 512]); hhB = [Buf() for _ in range(4)]
                hprev = sb(ph, "hprev", [128, 4]); hprevB = [Buf() for _ in range(4)]
                yrt = [sb(ph, "yrt%d" % i, [128, 4, 256], BF16) for i in range(2)]
                yrtB = [Buf(), Buf()]; yrtS = [P.dma_sem(), P.dma_sem()]
                cwp = sb(ph, "cwp", [128, 4, 4]); cwpB = Buf()
                rpp = sb(ph, "rpp", [128, 4, 4]); rppB = Buf()
                c8 = sb(ph, "c8", [128, 4, 2]); c8B = Buf()
                gab = sb(ph, "gab", [128, 4, 128]); gabB = Buf()
                gib = sb(ph, "gib", [128, 4, 128]); gibB = Buf()
                m0t = sb(ph, "m0t", [128, 128]); m0B = Buf()
                s0 = P.dma_sem()
                wv = w_in.rearrange("(k p) f -> p k f", p=128)
                ws_ = P.dma_sem()
                P.dma_group(ws_, [("pool", wrb[:, :, q * 512:(q + 1) * 512], wv[:, :, 1536 + q * 512:1536 + (q + 1) * 512], [wrB[q]])
                                  for q in range(2)])
                P.dma_group(s0, [("sp", cwp[:], conv_w_pp, [cwpB]), ("sp", rpp[:], rnn_pp, [rppB]), ("sp", gab[:], ga_bd, [gabB]),
                                 ("sp", gib[:], gi_bd, [gibB]), ("sp", m0t[:], m0, [m0B])])
                precast_ffn2_weights()
                P.op("act", lambda e: e.activation(out=c8[:, :, 0], in_=rpp[:, :, 3], func=AF.Exp, scale=-1.0), reads=[rppB], writes=[c8B])
                P.op("act", lambda e: e.activation(out=c8[:, :, 0], in_=c8[:, :, 0], func=AF.Ln, bias=1.0), reads=[c8B], writes=[c8B])
                P.op("dve", lambda e: e.tensor_scalar(out=c8[:, :, 1], in0=c8[:, :, 0], scalar1=-16.0, scalar2=None, op0=ALU.mult), reads=[c8B], writes=[c8B])
                P.op("dve", lambda e: e.tensor_scalar(out=c8[:, :, 0], in0=c8[:, :, 0], scalar1=-8.0, scalar2=None, op0=ALU.mult), reads=[c8B], writes=[c8B])
                dg = sb(ph, "dg", [128, 4, 4, 128]); dgB = Buf()
                for c in range(4):
                    for jt in range(4):
                        P.op("dve", lambda e, c=c, jt=jt: e.tensor_scalar(out=dg[:, c, jt, :], in0=ident[:], scalar1=cwp[:, c, jt:jt + 1],
                                                                          scalar2=None, op0=ALU.mult),
                             reads=[identB, cwpB], writes=[dgB])
                for c in range(4):
                    P.op("dve", lambda e, c=c: e.memset(xrt[0][:, c, 0:3], 0.0), writes=[xrtB[0][c]])
                    P.op("dve", lambda e, c=c: e.memset(hprev[:, c:c + 1], 0.0), writes=[hprevB[c]])

                def A_N(t):
                    for j in range(4):
                        gidx = t * 4 + j
                        slot = gidx % NSLOT
                        P.dma("sp", xr[slot][:], x1_s[gidx * 128:(gidx + 1) * 128, :], reads=[x1B[gidx]], writes=[xrB[slot]], sem=xrS[slot])
                        norm_part(xr[slot][:], xrB[slot], xn[j], xnB[j], ssq[j], ssqB[j], lnexp=True)

                def A_X(t):
                    for j in range(4):
                        tr_part(xn[j], xnB[j], 1, hT2[t % 2], hT2B[t % 2], j * 128, [psum[6], psum[7]], [psB[6], psB[7]])

                def A_T(t):
                    A_N(t)
                    A_X(t)

                def A_proj(t):
                    p = t % 2
                    hT_, hTB_ = hT2[p], hT2B[p]
                    for c in range(4):
                        bx, bxB = psum[c], psB[c]
                        bg, bgB = psum[4 + c % 2], psB[4 + c % 2]

                        def mmx(e, c=c, bx=bx):
                            ins = None
                            for k in range(KC):
                                ins = e.matmul(bx[:, :], lhsT=wrb[:, k, c * 128:(c + 1) * 128], rhs=hT_[:, k, :], start=(k == 0), stop=(k == KC - 1))
                            return ins
                        P.op("pe", mmx, reads=[wrB[0]] + hTB_, writes=[bxB])

                        def mmg(e, c=c, bg=bg):
                            ins = None
                            for jj in range(2):
                                for k in range(KC):
                                    ins = e.matmul(bg[:, jj * 128:(jj + 1) * 128], lhsT=wrb[:, k, 512 + c * 128:512 + (c + 1) * 128],
                                                   rhs=hT_[:, k, (2 * jj + 1) * 128:(2 * jj + 2) * 128], start=(k == 0), stop=(k == KC - 1))
                            return ins
                        P.op("pe", mmg, reads=[wrB[1]] + hTB_, writes=[bgB])
                        P.op("act", lambda e, c=c, bx=bx: e.copy(out=xrt[p][:, c, 3:515], in_=bx[:, :]), writes=[bxB, xrtB[p][c]])
                        P.op("act", lambda e, c=c, bg=bg: e.copy(out=grs[p][:, c, :], in_=bg[:, 0:256]), writes=[bgB, grsB[p][c]])
                        if t == 0:
                            P.op("dve", lambda e, c=c: e.tensor_tensor(out=xrt[p][:, c, 3:131], in0=xrt[p][:, c, 3:131], in1=m0t[:], op=ALU.mult),
                                 reads=[m0B], writes=[xrtB[p][c]])
                        P.op("dve", lambda e, c=c: e.tensor_copy(out=xrt[1 - p][:, c, 0:3], in_=xrt[p][:, c, 512:515]),
                             reads=[xrtB[p][c]], writes=[xrtB[1 - p][c]])

                def B_conv(t):
                    p = t % 2
                    for c in range(4):
                        def mmc(e, c=c):
                            ins = None
                            for jt in range(4):
                                ins = e.matmul(psum[c][:, :], lhsT=dg[:, c, jt, :], rhs=xrt[p][:, c, jt:jt + 512],
                                               start=(jt == 0), stop=(jt == 3))
                            return ins
                        P.op("pe", mmc, reads=[xrtB[p][c], dgB], writes=[psB[c]])
                        P.op("act", lambda e, c=c: e.activation(out=cvt[p][:, c, :], in_=psum[c][:, :], func=AF.Identity,
                                                                bias=rpp[:, c, 0:1]),
                             reads=[rppB], writes=[psB[c], cvtB[p][c]])

                def B_gates(t):
                    p = t % 2
                    P.op("dve", lambda e: e.tensor_tensor(out=gz[p][:], in0=grs[p][:], in1=grs[p][:], op=ALU.mult),
                         reads=grsB[p], writes=gzB[p])
                    P.op("dve", lambda e: e.tensor_scalar(out=gz[p][:], in0=gz[p][:], scalar1=0.044715, scalar2=1.0,
                                                          op0=ALU.mult, op1=ALU.add), writes=gzB[p])
                    P.op("dve", lambda e: e.tensor_tensor(out=gz[p][:], in0=gz[p][:], in1=grs[p][:], op=ALU.mult),
                         reads=grsB[p], writes=gzB[p])
                    for c in range(4):
                        bx, bxB = psum[c], psB[c]
                        bg, bgB = psum[4 + c % 2], psB[4 + c % 2]
                        P.op("pe", lambda e, c=c, bx=bx: e.matmul(bx[:, :], lhsT=gab[:, c, :], rhs=cvt[p][:, c, :], start=True, stop=True),
                             reads=[gabB, cvtB[p][c]], writes=[bxB])
                        P.op("pe", lambda e, c=c, bg=bg: e.matmul(bg[:, :], lhsT=gib[:, c, :], rhs=cvt[p][:, c, :], start=True, stop=True),
                             reads=[gibB, cvtB[p][c]], writes=[bgB])
                        P.op("act", lambda e, c=c, bx=bx: e.activation(out=rt[p][:, c, :], in_=bx[:, :], func=AF.Sigmoid, bias=rpp[:, c, 1:2]),
                             reads=[rppB], writes=[bxB, rtB[p][c]])
                        P.op("act", lambda e, c=c, bg=bg: e.activation(out=it[p][:, c, :], in_=bg[:, :], func=AF.Sigmoid, bias=rpp[:, c, 2:3]),
                             reads=[rppB], writes=[bgB, itB[p][c]])
                    P.op("act", lambda e: e.activation(out=gz[p][:], in_=gz[p][:], func=AF.Sigmoid, scale=1.5957691216057308), writes=gzB[p])
                    P.op("pool", lambda e: e.tensor_tensor(out=gz[p][:], in0=gz[p][:], in1=grs[p][:], op=ALU.mult),
                         reads=grsB[p], writes=gzB[p])

                def C_act(t):
                    p = t % 2
                    for c in range(4):
                        P.op("act", lambda e, c=c: e.activation(out=at[:, c, :], in_=rt[p][:, c, :], func=AF.Exp, scale=c8[:, c, 0:1]),
                             reads=[rtB[p][c], c8B], writes=[atB[c]])
                        P.op("act", lambda e, c=c: e.activation(out=qt_[:, c, :], in_=rt[p][:, c, :], func=AF.Exp, scale=c8[:, c, 1:2]),
                             reads=[rtB[p][c], c8B], writes=[qtB_[c]])
                    P.op("pool", lambda e: e.tensor_tensor(out=it[p][:], in0=it[p][:], in1=cvt[p][:], op=ALU.mult),
                         reads=cvtB[p], writes=itB[p])
                    P.op("act", lambda e: e.activation(out=qt_[:], in_=qt_[:], func=AF.Ln, scale=-1.0, bias=1.0), writes=qtB_)
                    P.op("act", lambda e: e.activation(out=qt_[:], in_=qt_[:], func=AF.Exp, scale=0.5), writes=qtB_)

                def C_dve(t):
                    p = t % 2
                    yslot = t % 2
                    P.op("pool", lambda e: e.tensor_tensor(out=it[p][:], in0=it[p][:], in1=qt_[:], op=ALU.mult),
                         reads=qtB_, writes=itB[p])
                    if t == 0:
                        for c in range(4):
                            P.op("pool", lambda e, c=c: e.tensor_tensor(out=it[p][:, c, 0:128], in0=it[p][:, c, 0:128], in1=m0t[:], op=ALU.mult),
                                 reads=[m0B], writes=[itB[p][c]])
                    for c in range(4):
                        P.op("dve", lambda e, c=c: e.tensor_tensor_scan(out=hh[:, c, :], data0=at[:, c, :], data1=it[p][:, c, :],
                                                                        initial=hprev[:, c:c + 1], op0=ALU.mult, op1=ALU.add),
                             reads=[atB[c], itB[p][c], hprevB[c]], writes=[hhB[c]])
                    P.op("dve", lambda e: e.tensor_copy(out=hprev[:, 0:4], in_=hh[:, :, 511]), reads=hhB, writes=hprevB)
                    for jj in range(2):
                        P.op("pool", lambda e, jj=jj: e.tensor_tensor(
                            out=yrt[yslot][:, :, jj * 128:(jj + 1) * 128], in0=hh[:, :, (2 * jj + 1) * 128:(2 * jj + 2) * 128],
                            in1=gz[p][:, :, jj * 128:(jj + 1) * 128], op=ALU.mult),
                            reads=hhB + gzB[p], writes=[yrtB[yslot]])
                    P.dma("sp", yr_s.rearrange("(c p) n -> p c n", p=128)[:, :, t * 256:(t + 1) * 256], yrt[yslot][:],
                          reads=[yrtB[yslot]], writes=[yrB[2 * t], yrB[2 * t + 1]], sem=yrtS[yslot])

                A_T(0)
                A_proj(0)
                for i in range(NT + 1):
                    if i >= 1:
                        C_act(i - 1)
                    if i < NT:
                        B_conv(i)
                    if i + 1 < NT:
                        A_N(i + 1)
                        A_X(i + 1)
                    if i < NT:
                        B_gates(i)
                    if i + 1 < NT:
                        A_proj(i + 1)
                    if i >= 1:
                        C_dve(i - 1)
                P.end_phase()

        if "p2b" in phases or "p3" in phases:
            with ExitStack() as kv:
                KT = sb(kv, "KT", [128, 4, S], BF16)
                KTB = [Buf() for _ in range(NT)]
                V = sb(kv, "V", [128, NBLK, 4, 129], BF16)
                VB = [Buf() for _ in range(NBLK)]
                VoneB = Buf()
                P.op("pool", lambda e: e.memset(V[:, :, :, 128:129], 1.0), writes=[VoneB])

                if "p2b" in phases:
                    with ExitStack() as ph:
                        wqb = sb(ph, "wqb", [128, KC, 1536], BF16); wqB = [Buf() for _ in range(3)]
                        NSLOT = 4
                        xr = [sb(ph, "kxr%d" % i, [128, D]) for i in range(NSLOT)]
                        xrB = [Buf() for _ in range(NSLOT)]
                        xrS = [P.dma_sem() for _ in range(NSLOT)]
                        xn = [sb(ph, "kxn%d" % i, [128, D]) for i in range(4)]; xnB = [Buf() for _ in range(4)]
                        ssq = [sb(ph, "kssq%d" % i, [128, 4]) for i in range(4)]; ssqB = [Buf() for _ in range(4)]
                        hT2 = [sb(ph, "khT%d" % i, [128, KC, 512], BF16) for i in range(2)]; hT2B = [[Buf() for _ in range(KC)] for _ in range(2)]
                        qst = [sb(ph, "qst%d" % i, [128, 4, 256], BF16) for i in range(2)]
                        qstB = [Buf(), Buf()]; qstS = [P.dma_sem(), P.dma_sem()]
                        wv = w_in.rearrange("(k p) f -> p k f", p=128)
                        ws_ = P.dma_sem()
                        P.dma_group(ws_, [("pool", wqb[:, :, q * 512:(q + 1) * 512], wv[:, :, q * 512:(q + 1) * 512], [wqB[q]])
                                          for q in range(3)])
                        gcnt = [0]
                        pc = 0

                        def kst_N(t):
                            for j in range(4):
                                gidx = t * 4 + j
                                slot = gidx % NSLOT
                                P.dma("sp", xr[slot][:], x1_s[gidx * 128:(gidx + 1) * 128, :], reads=[x1B[gidx]], writes=[xrB[slot]], sem=xrS[slot])
                                norm_part(xr[slot][:], xrB[slot], xn[j], xnB[j], ssq[j], ssqB[j])

                        def kst_X(t):
                            for j in range(4):
                                tr_part(xn[j], xnB[j], 1, hT2[t % 2], hT2B[t % 2], j * 128, [psum[6], psum[7]], [psB[6], psB[7]])

                        kst_N(0)
                        kst_X(0)
                        for t in range(NT):
                            if t + 1 < NT:
                                kst_N(t + 1)
                            hT, hTB = hT2[t % 2], hT2B[t % 2]
                            qs = t % 2
                            for c in range(4):
                                b = pc % 3
                                pc += 1

                                def mmk(e, c=c, b=b, hT=hT):
                                    ins = None
                                    for k in range(KC):
                                        ins = e.matmul(psum[b][:, :], lhsT=wqb[:, k, 512 + c * 128:512 + (c + 1) * 128], rhs=hT[:, k, :],
                                                       start=(k == 0), stop=(k == KC - 1))
                                    return ins
                                P.op("pe", mmk, reads=[wqB[1]] + hTB, writes=[psB[b]])
                                P.op("act", lambda e, c=c, b=b, t=t: e.copy(out=KT[:, c, t * 512:(t + 1) * 512], in_=psum[b][:, :]),
                                     writes=[psB[b], KTB[t]])
                                b2 = 3 + (pc % 3)

                                def mmq(e, c=c, b2=b2, hT=hT):
                                    ins = None
                                    for jj in range(2):
                                        for k in range(KC):
                                            ins = e.matmul(psum[b2][:, jj * 128:(jj + 1) * 128], lhsT=wqb[:, k, c * 128:(c + 1) * 128],
                                                           rhs=hT[:, k, (2 * jj + 1) * 128:(2 * jj + 2) * 128], start=(k == 0), stop=(k == KC - 1))
                                    return ins
                                P.op("pe", mmq, reads=[wqB[0]] + hTB, writes=[psB[b2]])
                                P.op("dve", lambda e, c=c, b2=b2, qs=qs: e.tensor_copy(out=qst[qs][:, c, :], in_=psum[b2][:, 0:256]),
                                     writes=[psB[b2], qstB[qs]])
                            for j in range(4):
                                b = pc % 3
                                pc += 1
                                blk = t * 4 + j

                                def mmv(e, j=j, b=b, hT=hT):
                                    ins = None
                                    for k in range(KC):
                                        ins = e.matmul(psum[b][:, :], lhsT=hT[:, k, j * 128:(j + 1) * 128], rhs=wqb[:, k, 1024:1536],
                                                       start=(k == 0), stop=(k == KC - 1))
                                    return ins
                                P.op("pe", mmv, reads=[wqB[2]] + hTB, writes=[psB[b]])
                                P.op("act" if j % 2 == 0 else "dve",
                                     (lambda e, b=b, blk=blk: e.copy(out=V[:, blk, :, 0:128], in_=psum[b][:, :].rearrange("p (c v) -> p c v", c=4)))
                                     if j % 2 == 0 else
                                     (lambda e, b=b, blk=blk: e.tensor_copy(out=V[:, blk, :, 0:128], in_=psum[b][:, :].rearrange("p (c v) -> p c v", c=4))),
                                     writes=[psB[b], VB[blk]])
                            if t + 1 < NT:
                                kst_X(t + 1)
                            P.dma("sp", qT_s.rearrange("(c p) n -> p c n", p=128)[:, :, t * 256:(t + 1) * 256], qst[qs][:],
                                  reads=[qstB[qs]], writes=[qTB[2 * t], qTB[2 * t + 1]], sem=qstS[qs])
                            if debug:
                                P.dma("sp", dbg["qT"].rearrange("(c p) n -> p c n", p=128)[:, :, t * 256:(t + 1) * 256], qst[qs][:],
                                      reads=[qstB[qs]], writes=[dbgB], sem=qstS[qs])
                        if debug:
                            P.dma("sp", dbg["kT"], KT[:], reads=KTB, writes=[dbgB], sem=P.dma_sem())
                            P.dma("sp", dbg["v"], V[:], reads=VB + [VoneB], writes=[dbgB], sem=P.dma_sem())
                        P.end_phase()

                if "p3" in phases:
                    with ExitStack() as ph:
                        wob = sb(ph, "wob", [128, KC, D], BF16); woB = [Buf(), Buf()]
                        g2b = sb(ph, "g2b", [128, D]); g2B = Buf()
                        bd = sb(ph, "bd", [128, 8, 128]); bdB = Buf()
                        bp = sb(ph, "bp", [128, 8, 128]); bpB = Buf()
                        bp0 = sb(ph, "bp0", [128, 8, 128]); bp0B = Buf()
                        cmt = sb(ph, "cmt", [128, 8]); cmB = Buf()
                        col0 = sb(ph, "col0", [128, 1]); col0B = Buf()
                        subg = sb(ph, "subg", [128, 128]); subgB = Buf()
                        qz = [[sb(ph, "qz%d_%d" % (i, e_), [128, 4, 512], BF16) for e_ in range(2)] for i in range(1)]
                        qzB = [Buf()]; qzS = [P.dma_sem()]
                        NPT = 5
                        pt = [sb(ph, "pt%d" % i, [128, 512], BF16) for i in range(NPT)]
                        ptB = [Buf() for _ in range(NPT)]
                        stmp = [sb(ph, "stmp%d" % i, [128, 128]) for i in range(2)]
                        stmpB = [Buf(), Buf()]
                        om = [sb(ph, "om%d" % i, [128, 4, 128]) for i in range(2)]
                        omB = [Buf(), Buf()]
                        rc = sb(ph, "rc", [128, 8]); rcB = Buf()
                        oh = sb(ph, "oh", [128, 4, 128]); ohB = Buf()
                        osq = sb(ph, "osq", [128, 4, 128]); osqB = Buf()
                        oss = sb(ph, "oss", [128, 12]); ossB = Buf()
                        oall = sb(ph, "oall", [128, 4, 512]); oallB = [Buf() for _ in range(4)]
                        oT = sb(ph, "oT", [128, 4, 128], BF16); oTB = Buf()
                        yT = [sb(ph, "yT%d" % i, [128, 4, 128], BF16) for i in range(2)]
                        yTB = [Buf(), Buf()]; yTS = [P.dma_sem(), P.dma_sem()]
                        x1t = [sb(ph, "x1t%d" % i, [128, D]) for i in range(2)]
                        x1tB = [Buf(), Buf()]; x1tS = [P.dma_sem(), P.dma_sem()]; x1oS = x1tS
                        mtmp = [sb(ph, "mtmp%d" % i, [128, 512]) for i in range(2)]
                        mtmpB = [Buf(), Buf()]
                        s0 = P.dma_sem()
                        ws_ = P.dma_sem()
                        wov = w_out.rearrange("(k p) f -> p k f", p=128)
                        P.dma_group(ws_, [("pool", wob[:, :, q * 512:(q + 1) * 512], wov[:, :, q * 512:(q + 1) * 512], [woB[q]])
                                          for q in range(2)])
                        P.dma("sp", g2b[:], gate_s[1], reads=[gateB[1]], writes=[g2B], sem=P.dma_sem())
                        P.dma_group(s0, [("sp", bd[:], bdm, [bdB]), ("sp", bp[:], bpm, [bpB]), ("sp", bp0[:], bp0m, [bp0B]),
                                         ("sp", cmt[:], cm_in, [cmB]), ("sp", col0[:], col0_in, [col0B]), ("sp", subg[:], subg_b, [subgB])])
                        for m in range(8):
                            for t_, tB_ in ((bd, bdB), (bp, bpB), (bp0, bp0B)):
                                P.op("dve", lambda e, t_=t_, m=m: e.tensor_scalar(out=t_[:, m, :], in0=t_[:, m, :], scalar1=cmt[:, m:m + 1],
                                                                                  scalar2=8.0, op0=ALU.subtract, op1=ALU.mult),
                                     reads=[cmB], writes=[tB_])
                        P.op("dve", lambda e: e.tensor_scalar(out=subg[:], in0=subg[:], scalar1=1.0 - LAMBDA_INIT, scalar2=None, op0=ALU.mult),
                             writes=[subgB])
                        for i in range(1):
                            P.op("pool", lambda e, i=i: e.memset(qz[i][0][64:128, :, :], 0.0), writes=[qzB[i]])
                            P.op("pool", lambda e, i=i: e.memset(qz[i][1][0:64, :, :], 0.0), writes=[qzB[i]])
                        Sb = [psum[0], psum[1], psum[2], psum[7]]; SbB = [psB[0], psB[1], psB[2], psB[7]]
                        Ob = [[psum[3], psum[4]], [psum[5], psum[6]]]; ObB = [[psB[3], psB[4]], [psB[5], psB[6]]]
                        qTv = qT_s.rearrange("(c p) n -> p c n", p=128)
                        yrv = yr_s.rearrange("(c p) n -> p c n", p=128)
                        sc_ = {"s": 0, "p": 0, "st": 0, "ep": 0}
                        LOOK = 3
                        pending = []
                        for g in range(NG):
                            qi = 0
                            P.dma("sp", qz[qi][0][0:64, :, :], qTv[0:64, :, g * 512:(g + 1) * 512],
                                  reads=[qTB[4 * g + i] for i in range(4)], writes=[qzB[qi]], sem=qzS[qi])
                            P.dma("sp", qz[qi][1][64:128, :, :], qTv[64:128, :, g * 512:(g + 1) * 512],
                                  reads=[qTB[4 * g + i] for i in range(4)], writes=[qzB[qi]], sem=qzS[qi])
                            nkb = 8 * g + 8
                            steps = [(c, e_, kb) for c in range(4) for e_ in range(2) for kb in range(nkb)]
                            info = {}

                            def emit_qk_exp(idx, g=g, qi=qi):
                                c, e_, kb = steps[idx]
                                m = 2 * c + e_
                                i = kb - 8 * g
                                s_lo = 0 if i < 0 else i // 2
                                ncols = (4 - s_lo) * 128
                                sn = sc_["s"] % 4
                                sc_["s"] += 1
                                pn = sc_["p"] % NPT
                                sc_["p"] += 1
                                info[idx] = (pn, s_lo)
                                P.op("pe", lambda e: e.matmul(
                                    Sb[sn][:, 0:ncols], lhsT=KT[:, c, kb * 128:(kb + 1) * 128], rhs=qz[qi][e_][:, c, s_lo * 128:512],
                                    start=True, stop=True),
                                    reads=[KTB[kb // 4], qzB[qi]], writes=[SbB[sn]])
                                if i >= 0:
                                    kind = (bp0 if kb == 0 else bp) if i % 2 == 0 else bd
                                    kindB = (bp0B if kb == 0 else bpB) if i % 2 == 0 else bdB
                                    P.op("dve", lambda e: e.tensor_tensor(out=Sb[sn][:, 0:128], in0=Sb[sn][:, 0:128], in1=kind[:, m, :], op=ALU.add),
                                         reads=[kindB], writes=[SbB[sn]])
                                if kb == 0:
                                    P.op("act", lambda e: e.activation(
                                        out=pt[pn][:, s_lo * 128:512], in_=Sb[sn][:, 0:ncols], func=AF.Exp, scale=0.125,
                                        bias=col0[:, 0:1]),
                                        reads=[col0B], writes=[SbB[sn], ptB[pn]])
                                else:
                                    P.op("act", lambda e: e.activation(
                                        out=pt[pn][:, s_lo * 128:512], in_=Sb[sn][:, 0:ncols], func=AF.Exp, scale=0.125),
                                        writes=[SbB[sn], ptB[pn]])

                            def emit_pv(idx, g=g, nkb=nkb):
                                c, e_, kb = steps[idx]
                                pn, s_lo = info.pop(idx)

                                def pv(e):
                                    ins = None
                                    for s in range(s_lo, 4):
                                        last = (kb == 8 * g + 2 * s + 1)
                                        o0 = (s % 2) * 256
                                        ins = e.matmul(Ob[e_][s // 2][:, o0:o0 + 129], lhsT=pt[pn][:, s * 128:(s + 1) * 128], rhs=V[:, kb, c, :],
                                                       start=(kb == 0 and s % 2 == 0), stop=last, skip_group_check=True)
                                    return ins
                                P.op("pe", pv, reads=[ptB[pn], VB[kb], VoneB], writes=[ObB[e_][b_] for b_ in range(s_lo // 2, 2)])
                                if kb != nkb - 1:
                                    return
                                for b_ in range(2):
                                    P.op("dve", lambda e, b_=b_: e.reciprocal(out=rc[:, 4 * e_ + 2 * b_:4 * e_ + 2 * b_ + 2],
                                                                              in_=Ob[e_][b_][:, 128:512:256]),
                                         writes=[ObB[e_][b_], rcB])
                                    for s in (2 * b_, 2 * b_ + 1):
                                        o0 = (s % 2) * 256
                                        P.op("dve", lambda e, s=s, o0=o0, b_=b_: e.tensor_scalar(
                                            out=om[e_][:, s, :], in0=Ob[e_][b_][:, o0:o0 + 128],
                                            scalar1=rc[:, 4 * e_ + s:4 * e_ + s + 1], scalar2=None, op0=ALU.mult),
                                            reads=[rcB], writes=[ObB[e_][b_], omB[e_]])
                                if e_ != 1:
                                    return
                                assert not pending
                                P.op("dve", lambda e: e.scalar_tensor_tensor(out=oh[:], in0=om[1][:], scalar=neglam[:, 0:1], in1=om[0][:],
                                                                             op0=ALU.mult, op1=ALU.add),
                                     reads=[omB[0], omB[1], neglamB], writes=[ohB])
                                P.op("pool", lambda e: e.tensor_tensor(out=osq[:], in0=oh[:], in1=oh[:], op=ALU.mult), reads=[ohB], writes=[osqB])
                                P.op("dve", lambda e: e.tensor_reduce(out=oss[:, 0:4], in_=osq[:], axis=AX.X, op=ALU.add), reads=[osqB], writes=[ossB])
                                P.op("dve", lambda e: e.tensor_scalar(out=oss[:, 4:8], in0=oss[:, 0:4], scalar1=1.0 / 128, scalar2=EPS,
                                                                      op0=ALU.mult, op1=ALU.add), writes=[ossB])
                                P.op("pool", lambda e: e.tensor_tensor(out=oss[:, 8:12], in0=oss[:, 4:8], in1=neghalf4[:, 0:4], op=ALU.pow),
                                     reads=[neghalfB], writes=[ossB])
                                for s in range(4):
                                    P.op("dve", lambda e, s=s: e.scalar_tensor_tensor(
                                        out=oall[:, s, c * 128:(c + 1) * 128], in0=oh[:, s, :], scalar=oss[:, 8 + s:9 + s], in1=subg[:],
                                        op0=ALU.mult, op1=ALU.mult),
                                        reads=[ohB, ossB, subgB], writes=[oallB[s]])

                            nst = len(steps)
                            for idx in range(nst + LOOK):
                                if pending and idx % 2 == 1:
                                    pending.pop(0)()
                                if idx < nst:
                                    emit_qk_exp(idx)
                                if idx >= LOOK:
                                    emit_pv(idx - LOOK)
                            assert not pending
                            def make_pieces(g):
                                pieces = []
                                loads = []
                                for s in range(4):
                                    ob = 4 * g + s
                                    en = sc_["ep"] % 2
                                    sc_["ep"] += 1
                                    pblk = 2 * ob + 1

                                    def p_load(ob=ob, en=en, pblk=pblk):
                                        P.dma("sp", yT[en][:], yrv[:, :, ob * 128:(ob + 1) * 128], reads=[yrB[ob]], writes=[yTB[en]], sem=yTS[en])
                                        P.dma("sp", x1t[en][:], x1_s[pblk * 128:(pblk + 1) * 128, :], reads=[x1B[pblk]], writes=[x1tB[en]],
                                              sem=x1tS[en])
                                    loads.append(p_load)

                                    def p_tr(s=s, ob=ob, en=en, pblk=pblk):
                                        if s == 0:
                                            loads[0]()
                                        if s + 1 < 4:
                                            loads[s + 1]()

                                        bn = sc_["s"] % 4
                                        sc_["s"] += 1
                                        Tb, TbB = Sb[bn], SbB[bn]

                                        def tro(e):
                                            ins = None
                                            for c in range(4):
                                                ins = e.transpose(out=Tb[:, c * 128:(c + 1) * 128], in_=oall[:, s, c * 128:(c + 1) * 128],
                                                                  identity=ident[:])
                                            return ins
                                        P.op("pe", tro, reads=[oallB[s], identB], writes=[TbB])
                                        P.op("dve", lambda e: e.tensor_copy(out=oT[:], in_=Tb[:, :].rearrange("p (c t) -> p c t", c=4)),
                                             writes=[TbB, oTB])
                                    pieces.append(p_tr)
                                    for half in range(2):
                                        def p_mm(half=half, en=en, ob=ob):
                                            bn = sc_["s"] % 4
                                            sc_["s"] += 1
                                            Mb, MbB = Sb[bn], SbB[bn]

                                            def mmo(e):
                                                ins = None
                                                for c in range(4):
                                                    ins = e.matmul(Mb[:, :], lhsT=oT[:, c, :], rhs=wob[:, c, half * 512:(half + 1) * 512],
                                                                   start=(c == 0), stop=False)
                                                for c in range(4):
                                                    ins = e.matmul(Mb[:, :], lhsT=yT[en][:, c, :], rhs=wob[:, 4 + c, half * 512:(half + 1) * 512],
                                                                   start=False, stop=(c == 3))
                                                return ins
                                            P.op("pe", mmo, reads=[oTB, yTB[en], woB[half]], writes=[MbB])
                                            P.op("dve", lambda e: e.tensor_tensor(out=mtmp[half][:], in0=Mb[:, :],
                                                                                  in1=g2b[:, half * 512:(half + 1) * 512], op=ALU.mult),
                                                 reads=[g2B], writes=[MbB, mtmpB[half]])
                                            P.op("pool", lambda e: e.tensor_tensor(
                                                out=x1t[en][:, half * 512:(half + 1) * 512], in0=x1t[en][:, half * 512:(half + 1) * 512],
                                                in1=mtmp[half][:], op=ALU.add),
                                                reads=[mtmpB[half]], writes=[x1tB[en]])
                                            if half == 1:
                                                P.dma("sp", x2_s[ob * 128:(ob + 1) * 128, :], x1t[en][:], reads=[x1tB[en]], writes=[x2B[ob]],
                                                      sem=x1oS[en])
                                        pieces.append(p_mm)
                                return pieces
                            pending = make_pieces(g)
                        for f_ in pending:
                            f_()
                        P.end_phase()

        if "p4" in phases:
            ffn_phase("b", NOWN, lambda g: x2_s[g * 128:(g + 1) * 128, :], lambda g: x2B[g],
                      lambda g: out[g * 128:(g + 1) * 128, :], lambda g: outB,
                      f2w1, f2w3, f2w2, 2, 2, True, pre=(w1s, w3s, w2s, wsB) if "p2a" in phases else None)
        P.barrier()
        P.flush()
    return nc


def _t5_bucket_np(rel):
    n = np.maximum(rel, 0)
    max_exact = 16
    nf = np.maximum(n, 1).astype(np.float32)
    large = max_exact + (np.log(nf / max_exact) / math.log(128 / max_exact) * (32 - max_exact)).astype(np.int32)
    large = np.minimum(large, 31)
    return np.where(n < max_exact, n, large)


def make_in_maps(inputs, NBLK=64, batch_ids=None, n_cores=8):
    f32 = np.float32
    g = {k: np.asarray(v) for k, v in inputs.items()}
    S = NBLK * 128
    x = g["x"]
    shared = {}
    shared["ada_w"] = np.ascontiguousarray(g["ada_w"][0], dtype=f32)
    ada_b = g["ada_b"][0].astype(f32)
    shared["ada_b_pp"] = np.ascontiguousarray(ada_b.reshape(72, 128).T)
    shared["ada_b_gate"] = np.ascontiguousarray(np.broadcast_to(ada_b.reshape(9, D)[[2, 5, 8]][None], (128, 3, D)))
    shared["norm_g_pp"] = np.ascontiguousarray(g["norm_g"][0].astype(f32).reshape(3, KC, 128).transpose(2, 0, 1))
    shared["final_g_b"] = np.ascontiguousarray(np.broadcast_to(g["final_g"].astype(f32)[None], (128, D)))
    for a, b in (("f1w1", "ffn1_w1"), ("f1w3", "ffn1_w3"), ("f1w2", "ffn1_w2"),
                 ("f2w1", "ffn2_w1"), ("f2w3", "ffn2_w3"), ("f2w2", "ffn2_w2"), ("w_in", "w_in"), ("w_out", "w_out")):
        shared[a] = np.ascontiguousarray(g[b][0], dtype=f32)
    lam = np.stack([g["lam_q1"][0], g["lam_k1"][0], g["lam_q2"][0], g["lam_k2"][0]]).astype(f32)
    shared["lamv"] = np.ascontiguousarray(np.broadcast_to(lam[None], (128, 4, 64)))
    shared["subg_b"] = np.ascontiguousarray(np.broadcast_to(g["subln_g"][0].astype(f32)[None], (128, 128)))
    shared["conv_w_pp"] = np.ascontiguousarray(g["conv_w"][0].astype(f32).reshape(4, 4, 128).transpose(2, 1, 0))
    rp = np.stack([g["conv_b"][0], g["gate_a_b"][0], g["gate_i_b"][0], g["lru_L"][0]]).astype(f32)
    shared["rnn_pp"] = np.ascontiguousarray(rp.reshape(4, 4, 128).transpose(2, 1, 0))
    for nm, key in (("ga_bd", "gate_a_w"), ("gi_bd", "gate_i_w")):
        w = g[key][0].astype(f32)
        bdm = np.zeros((128, 4, 128), f32)
        for c in range(4):
            for hb in range(2):
                bdm[hb * 64:(hb + 1) * 64, c, hb * 64:(hb + 1) * 64] = w[2 * c + hb]
        shared[nm] = bdm
    rb = g["rel_bias"].astype(f32)
    kk = np.arange(128)[:, None]
    qq = np.arange(128)[None, :]
    rel_d = qq - kk
    rel_p = qq - kk + 128
    bd = rb[_t5_bucket_np(rel_d)]
    bd = np.where((rel_d >= 0)[:, :, None], bd, f32(NEG))
    bp = rb[_t5_bucket_np(rel_p)]
    shared["bdm"] = np.ascontiguousarray(bd.transpose(0, 2, 1).astype(f32))
    shared["bpm"] = np.ascontiguousarray(bp.transpose(0, 2, 1).astype(f32))
    shared["cm"] = np.ascontiguousarray(np.broadcast_to(rb[31][None], (128, 8)))
    if batch_ids is None:
        batch_ids = [c // 2 for c in range(n_cores)]
    maps = []
    for core in range(n_cores):
        b = batch_ids[core]
        h = core % 2
        m = dict(shared)
        xs = np.zeros((S, D), f32)
        if h == 0:
            xs[128:] = x[b, :S - 128]
            m["m0"] = np.zeros((128, 128), f32)
            m["bp0m"] = np.full((128, 8, 128), NEG, f32)
            m["col0"] = np.full((128, 1), NEG, f32)
        else:
            xs[:] = x[b, :S]
            m["m0"] = np.ones((128, 128), f32)
            m["bp0m"] = shared["bpm"]
            m["col0"] = np.zeros((128, 1), f32)
        m["xs"] = xs
        m["cvec"] = np.ascontiguousarray(g["c"][b].astype(f32).reshape(KC, 128).T)
        maps.append(m)
    return maps


_NC_CACHE = {}


def kernel(**inputs):
    NBLK = 64
    if NBLK not in _NC_CACHE:
        _NC_CACHE[NBLK] = build(NBLK)
    nc = _NC_CACHE[NBLK]
    maps = make_in_maps(inputs, NBLK)
    res = run_bass_kernel_spmd(nc, maps, core_ids=list(range(8)))
    B = 4
    S = 8192
    outp = np.zeros((B, S, D), np.float32)
    for core in range(8):
        b, h = core // 2, core % 2
        o = np.asarray(res.results[core]["out"]).reshape(32, 128, D)
        outp[b].reshape(64, 128, D)[h::2] = o
    return outp
```
